# Optimizing a Trainium2 kernel written in Bass

```python
import math
import jax, jax.numpy as jnp
from jax import lax
import numpy as np

D_MODEL = 1024
BATCH = 8
SEQ = 4096
DEPTH = 1

MLA_HEADS = 8
MLA_Q_LORA = 256
MLA_KV_LORA = 128
MLA_NOPE = 64
MLA_ROPE = 32
MLA_V = 64
ROPE_THETA = 10000.0
Q_BLOCK = 128
GDN_HEADS = 4
GDN_DK = 128
GDN_DV = 128
GDN_CONV = 5
GDN_CHUNK = 64
D_FF = 4 * D_MODEL
EPS = 1e-6

MLA_WIDTH = MLA_HEADS * MLA_V
GDN_QK = GDN_HEADS * GDN_DK
GDN_WIDTH = GDN_HEADS * GDN_DV
GDN_QKV = 2 * GDN_QK + GDN_WIDTH
IN_SPLITS = (MLA_Q_LORA, MLA_KV_LORA, MLA_ROPE, GDN_QKV,
             GDN_HEADS, GDN_HEADS, GDN_HEADS, GDN_HEADS,
             GDN_WIDTH, D_MODEL, D_MODEL)
D_IN = (MLA_Q_LORA + MLA_KV_LORA + MLA_ROPE + GDN_QKV + 4 * GDN_HEADS
        + GDN_WIDTH + 2 * D_MODEL)

kernel_name = "hybrid_mla_gdn_gated_merge_encoder"


def _rmsnorm(x, w):
    xf = x.astype(jnp.float32)
    y = xf * lax.rsqrt(jnp.mean(xf * xf, axis=-1, keepdims=True) + EPS)
    return (y * w.astype(jnp.float32)).astype(x.dtype)


def _l2norm(x):
    xf = x.astype(jnp.float32)
    return xf * lax.rsqrt(jnp.sum(xf * xf, axis=-1, keepdims=True) + EPS)


def _rope(x, positions):
    d = x.shape[-1]
    inv_freq = 1.0 / (ROPE_THETA ** (jnp.arange(0, d, 2, dtype=jnp.float32) / d))
    ang = positions.astype(jnp.float32)[..., None] * inv_freq
    ang = ang.reshape(ang.shape[:2] + (1,) * (x.ndim - 3) + (d // 2,))
    cos, sin = jnp.cos(ang), jnp.sin(ang)
    xf = x.astype(jnp.float32)
    x1, x2 = xf[..., : d // 2], xf[..., d // 2:]
    out = jnp.concatenate([x1 * cos - x2 * sin, x2 * cos + x1 * sin], axis=-1)
    return out.astype(x.dtype)


def _mla_attention(q_nope, q_rope, k_nope, k_rope, v):
    B, S, H, _ = q_nope.shape
    nb = S // Q_BLOCK
    qn = q_nope.reshape(B, nb, Q_BLOCK, H, MLA_NOPE).swapaxes(0, 1)
    qr = q_rope.reshape(B, nb, Q_BLOCK, H, MLA_ROPE).swapaxes(0, 1)
    scale = 1.0 / math.sqrt(MLA_NOPE + MLA_ROPE)

    def block(args):
        qn_b, qr_b = args
        s = (jnp.einsum('bqhd,bkhd->bhqk', qn_b, k_nope)
             + jnp.einsum('bqhd,bkd->bhqk', qr_b, k_rope))
        p = jax.nn.softmax(s.astype(jnp.float32) * scale, axis=-1).astype(v.dtype)
        return jnp.einsum('bhqk,bkhd->bqhd', p, v)

    o = lax.map(block, (qn, qr))
    return o.swapaxes(0, 1).reshape(B, S, H * MLA_V)


def _to_chunks(t, n, c):
    t = t.reshape((t.shape[0], n, c) + t.shape[2:])
    return jnp.moveaxis(t, 2, 3)


def _gated_delta_chunked(q, k, v, g, beta):
    B, S, H, DK = q.shape
    DV = v.shape[-1]
    C = GDN_CHUNK
    N = S // C
    qc, kc, vc = (_to_chunks(t.astype(jnp.float32), N, C) for t in (q, k, v))
    gc = jnp.cumsum(_to_chunks(g.astype(jnp.float32), N, C), axis=-1)
    bc = _to_chunks(beta.astype(jnp.float32), N, C)

    tril_incl = jnp.tril(jnp.ones((C, C), dtype=bool))
    tril_strict = jnp.tril(jnp.ones((C, C), dtype=bool), -1)
    diff = gc[..., :, None] - gc[..., None, :]
    decay = jnp.where(tril_incl, jnp.exp(jnp.where(tril_incl, diff, 0.0)), 0.0)

    k_beta = kc * bc[..., None]
    v_beta = vc * bc[..., None]
    lower = jnp.where(tril_strict, jnp.einsum('bnhid,bnhjd->bnhij', k_beta, kc) * decay, 0.0)
    a_mat = lower + jnp.eye(C, dtype=jnp.float32)
    rhs = jnp.concatenate([v_beta, k_beta * jnp.exp(gc)[..., None]], axis=-1)
    sol = lax.linalg.triangular_solve(a_mat, rhs, left_side=True, lower=True,
                                      unit_diagonal=True)
    u, w = sol[..., :DV], sol[..., DV:]
    qk = jnp.einsum('bnhid,bnhjd->bnhij', qc, kc) * decay

    def step(state, inp):
        q_i, k_i, u_i, w_i, g_i, qk_i = inp
        v_new = u_i - jnp.einsum('bhcd,bhde->bhce', w_i, state)
        o = (jnp.einsum('bhcd,bhde->bhce', q_i * jnp.exp(g_i)[..., None], state)
             + jnp.einsum('bhij,bhje->bhie', qk_i, v_new))
        g_last = g_i[..., -1]
        k_dec = k_i * jnp.exp(g_last[..., None] - g_i)[..., None]
        state = (state * jnp.exp(g_last)[..., None, None]
                 + jnp.einsum('bhcd,bhce->bhde', k_dec, v_new))
        return state, o

    xs = tuple(jnp.moveaxis(t, 1, 0) for t in (qc, kc, u, w, gc, qk))
    state0 = jnp.zeros((B, H, DK, DV), jnp.float32)
    _, o = lax.scan(step, state0, xs)
    o = jnp.moveaxis(o, 0, 1).swapaxes(2, 3)
    return o.reshape(B, S, H, DV)


def setup_inputs(seed: int = 0) -> dict:
    key = jax.random.key(seed)
    ks = jax.random.split(key, 24)
    f32 = jnp.float32

    def nrm(k, shape, fan_in):
        return jax.random.normal(k, shape, f32) * (fan_in ** -0.5)

    def gain(k, shape):
        return 1.0 + 0.01 * jax.random.normal(k, shape, f32)

    x = jax.random.normal(ks[0], (BATCH, SEQ, D_MODEL), f32)
    offset = jax.random.randint(ks[1], (BATCH, 1), 0, 1024, dtype=jnp.int32)
    positions = (jnp.arange(SEQ, dtype=jnp.int32)[None, :] + offset).astype(jnp.int32)

    a_f = jax.random.uniform(ks[10], (DEPTH, GDN_HEADS), f32, 1.0, 16.0)
    a_b = jax.random.uniform(ks[11], (DEPTH, GDN_HEADS), f32, 1.0, 16.0)
    dt_f = jnp.exp(jax.random.uniform(ks[12], (DEPTH, GDN_HEADS), f32, math.log(1e-3), math.log(1e-1)))
    dt_b = jnp.exp(jax.random.uniform(ks[13], (DEPTH, GDN_HEADS), f32, math.log(1e-3), math.log(1e-1)))

    return {
        "x": x,
        "positions": positions,
        "norm1_w": gain(ks[2], (DEPTH, D_MODEL)),
        "w_in": nrm(ks[3], (DEPTH, D_MODEL, D_IN), D_MODEL),
        "q_norm_w": gain(ks[4], (DEPTH, MLA_Q_LORA)),
        "w_uq": nrm(ks[5], (DEPTH, MLA_Q_LORA, MLA_HEADS * (MLA_NOPE + MLA_ROPE)), MLA_Q_LORA),
        "kv_norm_w": gain(ks[6], (DEPTH, MLA_KV_LORA)),
        "w_ukv": nrm(ks[7], (DEPTH, MLA_KV_LORA, MLA_HEADS * (MLA_NOPE + MLA_V)), MLA_KV_LORA),
        "conv_w": nrm(ks[8], (DEPTH, GDN_CONV, GDN_QKV), GDN_CONV),
        "a_log_f": jnp.log(a_f),
        "dt_bias_f": dt_f + jnp.log(-jnp.expm1(-dt_f)),
        "a_log_b": jnp.log(a_b),
        "dt_bias_b": dt_b + jnp.log(-jnp.expm1(-dt_b)),
        "gdn_norm_w": gain(ks[9], (DEPTH, GDN_DV)),
        "w_proj_a": nrm(ks[14], (DEPTH, MLA_WIDTH, D_MODEL), MLA_WIDTH),
        "w_proj_b": nrm(ks[15], (DEPTH, GDN_WIDTH, D_MODEL), GDN_WIDTH),
        "w_out": nrm(ks[16], (DEPTH, D_MODEL, D_MODEL), D_MODEL),
        "norm2_w": gain(ks[17], (DEPTH, D_MODEL)),
        "w_ff1": nrm(ks[18], (DEPTH, D_MODEL, D_FF), D_MODEL),
        "w_ff2": nrm(ks[19], (DEPTH, D_FF, D_MODEL), D_FF),
        "final_norm_w": gain(ks[20], (D_MODEL,)),
    }


def reference(x, positions, norm1_w, w_in, q_norm_w, w_uq, kv_norm_w, w_ukv, conv_w,
              a_log_f, dt_bias_f, a_log_b, dt_bias_b, gdn_norm_w, w_proj_a, w_proj_b,
              w_out, norm2_w, w_ff1, w_ff2, final_norm_w):
    B, S, _ = x.shape
    split_idx = [int(v) for v in np.cumsum(IN_SPLITS)[:-1]]

    for l in range(DEPTH):
        h = _rmsnorm(x, norm1_w[l])
        z = h @ w_in[l]
        (c_q, c_kv, k_r, gdn_qkv, ga_f, ga_b, gb_f, gb_b,
         gdn_gate, gate_a, gate_b) = jnp.split(z, split_idx, axis=-1)

        q = (_rmsnorm(c_q, q_norm_w[l]) @ w_uq[l]).reshape(B, S, MLA_HEADS, MLA_NOPE + MLA_ROPE)
        q_nope, q_rope = q[..., :MLA_NOPE], _rope(q[..., MLA_NOPE:], positions)
        kv = (_rmsnorm(c_kv, kv_norm_w[l]) @ w_ukv[l]).reshape(B, S, MLA_HEADS, MLA_NOPE + MLA_V)
        k_nope, v_a = kv[..., :MLA_NOPE], kv[..., MLA_NOPE:]
        k_rope = _rope(k_r, positions)
        o_a = _mla_attention(q_nope, q_rope, k_nope, k_rope, v_a)

        cw = conv_w[l].astype(gdn_qkv.dtype)[:, None, :]
        qkv = lax.conv_general_dilated(gdn_qkv, cw, window_strides=(1,),
                                       padding=((GDN_CONV // 2, GDN_CONV // 2),),
                                       dimension_numbers=('NWC', 'WIO', 'NWC'),
                                       feature_group_count=GDN_QKV)
        qkv = jax.nn.silu(qkv)
        gq = _l2norm(qkv[..., :GDN_QK].reshape(B, S, GDN_HEADS, GDN_DK)) * (GDN_DK ** -0.5)
        gk = _l2norm(qkv[..., GDN_QK:2 * GDN_QK].reshape(B, S, GDN_HEADS, GDN_DK))
        gv = qkv[..., 2 * GDN_QK:].reshape(B, S, GDN_HEADS, GDN_DV).astype(jnp.float32)
        g_f = -jnp.exp(a_log_f[l].astype(jnp.float32)) * jax.nn.softplus(
            ga_f.astype(jnp.float32) + dt_bias_f[l].astype(jnp.float32))
        g_b = -jnp.exp(a_log_b[l].astype(jnp.float32)) * jax.nn.softplus(
            ga_b.astype(jnp.float32) + dt_bias_b[l].astype(jnp.float32))
        beta_f = jax.nn.sigmoid(gb_f.astype(jnp.float32))
        beta_b = jax.nn.sigmoid(gb_b.astype(jnp.float32))
        o_fwd = _gated_delta_chunked(gq, gk, gv, g_f, beta_f)
        o_bwd = jnp.flip(_gated_delta_chunked(jnp.flip(gq, 1), jnp.flip(gk, 1), jnp.flip(gv, 1),
                                              jnp.flip(g_b, 1), jnp.flip(beta_b, 1)), 1)
        o_b = _rmsnorm(o_fwd + o_bwd, gdn_norm_w[l]) * jax.nn.silu(
            gdn_gate.reshape(B, S, GDN_HEADS, GDN_DV).astype(jnp.float32))
        o_b = o_b.reshape(B, S, GDN_WIDTH).astype(x.dtype)

        merged = (jax.nn.sigmoid(gate_a) * (o_a @ w_proj_a[l])
                  + jax.nn.sigmoid(gate_b) * (o_b @ w_proj_b[l]))
        x = x + merged @ w_out[l]

        h2 = _rmsnorm(x, norm2_w[l])
        x = x + jnp.square(jax.nn.relu(h2 @ w_ff1[l])) @ w_ff2[l]

    return _rmsnorm(x, final_norm_w)
```

```python
import math
from contextlib import ExitStack

import numpy as np
import concourse.bass as bass
import concourse.mybir as mybir
from concourse.bass_utils import run_bass_kernel_spmd

F32 = mybir.dt.float32
BF16 = mybir.dt.bfloat16
I32 = mybir.dt.int32
AF = mybir.ActivationFunctionType
ALU = mybir.AluOpType
AX = mybir.AxisListType

D = 1024
D_IN = 4528
DFF = 4096
EPS = 1e-6
C_Q, C_KV, C_KR, C_QKV, C_G, C_GG, C_GA, C_GB = 0, 256, 384, 416, 1952, 1968, 2480, 3504

COMPUTE = ("pe", "act", "dve", "pool")
QUEUES = ("sp",)


class Buf:
    __slots__ = ("w", "r", "excl")

    def __init__(self, excl=False):
        self.w = {}
        self.r = {}
        self.excl = excl


class Prog:
    def __init__(self, nc):
        self.nc = nc
        self.ins = {e: [] for e in COMPUTE + QUEUES}
        self.tags = {}
        self.tag_list = []
        self.emitted = {e: 0 for e in COMPUTE + QUEUES}
        self.waited = {e: {} for e in COMPUTE + QUEUES}
        self.cnt = {e: 0 for e in COMPUTE}
        self.vals = {e: [] for e in COMPUTE}
        self.esem = None
        self.tsem = {}
        self.semstack = None
        self.tagctr = 0

    def new_tag(self):
        self.tagctr += 1
        return "q%d" % self.tagctr

    def _deps(self, reads, writes):
        deps = []
        for b in reads:
            deps.extend(b.w.values())
        for b in writes:
            deps.extend(b.w.values())
            deps.extend(b.r.values())
        return deps

    def op(self, eng, fn, reads=(), writes=()):
        if any(b.excl for b in reads):
            writes = list(writes) + [b for b in reads if b.excl]
            reads = [b for b in reads if not b.excl]
        deps = self._deps(reads, writes)
        idx = len(self.ins[eng])
        ev = ("e", eng, idx)
        self.ins[eng].append({"fn": fn, "deps": deps, "kind": "c", "need": False})
        for b in reads:
            b.r[eng] = ev
        for b in writes:
            b.w = {eng: ev}
            b.r = {}
        return ev

    def dma(self, q, fn, tag, reads=(), writes=()):
        if q != "sp":
            tag = "sw_" + str(tag)
        deps = self._deps(reads, writes)
        if tag not in self.tags:
            self.tags[tag] = 0
            self.tag_list.append(tag)
        self.tags[tag] += 1
        ev = ("d", tag, self.tags[tag])
        self.ins[q].append({"fn": fn, "deps": deps, "kind": "d", "tag": tag, "need": False})
        key = ("d", tag)
        for b in reads:
            b.r[key] = ev
        for b in writes:
            b.w = {key: ev}
            b.r = {}
        return ev

    def barrier(self):
        last = []
        for e in COMPUTE:
            if self.ins[e]:
                for i in range(len(self.ins[e]) - 1, -1, -1):
                    if self.ins[e][i]["kind"] == "c":
                        last.append(("e", e, i))
                        break
        for t in self.tag_list:
            last.append(("d", t, self.tags[t]))
        for e in COMPUTE + QUEUES:
            self.ins[e].append({"fn": None, "deps": list(last), "kind": "b", "need": False})
        self.tagctr = 0

    def emit(self, final=False):
        nc = self.nc
        if self.semstack is None:
            self.semstack = ExitStack()
            self.esem = {e: self.semstack.enter_context(nc.semaphore("s_" + e)) for e in COMPUTE}
        for t in self.tag_list:
            if t not in self.tsem:
                self.tsem[t] = self.semstack.enter_context(nc.semaphore("t_" + str(t)))
        for e, lst in self.ins.items():
            for rec in lst[self.emitted[e]:]:
                for d in rec["deps"]:
                    if d[0] == "e" and not (d[1] == e and e == "pe"):
                        self.ins[d[1]][d[2]]["need"] = True
        for e in COMPUTE:
            lst = self.ins[e]
            lastc = None
            for i in range(self.emitted[e], len(lst)):
                if lst[i]["kind"] == "c":
                    lastc = i
            for i in range(self.emitted[e], len(lst)):
                rec = lst[i]
                if rec["kind"] == "c" and (rec["need"] or i == lastc):
                    rec["need"] = True
                    self.cnt[e] += 1
                self.vals[e].append(self.cnt[e])

        def run(e, engobj):
            waited = self.waited[e]
            lst = self.ins[e]
            for i in range(self.emitted[e], len(lst)):
                rec = lst[i]
                for d in rec["deps"]:
                    if d[0] == "e":
                        if d[1] == e and e == "pe":
                            continue
                        val = self.vals[d[1]][d[2]]
                        sem, key = self.esem[d[1]], d[1]
                    else:
                        sem, val, key = self.tsem[d[1]], 16 * d[2], ("d", d[1])
                    if waited.get(key, 0) >= val:
                        continue
                    waited[key] = val
                    engobj.wait_ge(sem, val)
                if rec["kind"] == "b":
                    continue
                inst = rec["fn"](engobj)
                if rec["kind"] == "d":
                    inst.then_inc(self.tsem[rec["tag"]], 16)
                elif rec["need"]:
                    inst.then_inc(self.esem[e], 1)
            if final and e == "sp":
                for t in self.tag_list:
                    val = 16 * self.tags[t]
                    if waited.get(("d", t), 0) < val:
                        engobj.wait_ge(self.tsem[t], val)
            self.emitted[e] = len(lst)

        with nc.Block() as block:
            block.tensor(lambda eng: run("pe", eng))
            block.scalar(lambda eng: run("act", eng))
            block.vector(lambda eng: run("dve", eng))
            block.gpsimd(lambda eng: run("pool", eng))
            block.sync(lambda eng: run("sp", eng))
        if final:
            self.semstack.close()


CONST_NAMES = ["ident", "ones", "MLi", "MUi", "NOTEYE", "BD", "BIGf", "BIGb", "misc"]
BIG = 30000.0


def make_consts():
    p = np.arange(128)[:, None]
    f = np.arange(128)[None, :]
    same = (p // 64) == (f // 64)
    c = {}
    c["ident"] = (p == f)
    c["ones"] = np.ones((128, 128))
    c["MLi"] = (p >= f) & same
    c["MUi"] = (p <= f) & same
    c["NOTEYE"] = (p != f)
    c["BD"] = same
    c["BIGf"] = BIG * (~((p >= f) & same))
    c["BIGb"] = BIG * (~((p <= f) & same))
    misc = np.zeros((128, 128))
    misc[:, 0] = (np.arange(128) < 64)
    misc[:, 1] = (np.arange(128) >= 64)
    inv_freq = 1.0 / (10000.0 ** (np.arange(0, 32, 2, dtype=np.float32) / 32.0))
    misc[:, 2] = np.tile(inv_freq.astype(np.float32), 8)
    misc[64, 64:128] = 1.0
    c["misc"] = misc
    return np.concatenate([np.asarray(c[n], dtype=np.float32) for n in CONST_NAMES], axis=1)


def build(T=4096, dbg=()):
    NT = T // 128
    NB = T // 512
    nc = bass.Bass("TRN2", target_bir_lowering=False)

    def din(name, shape, dt=F32):
        return nc.dram_tensor(name, list(shape), dt, kind="ExternalInput").ap()

    x = din("x", [T, D])
    pos = din("positions", [1, T], I32)
    norm1_w = din("norm1_w", [D])
    w_in = din("w_in", [D, D_IN])
    q_norm_w = din("q_norm_w", [256])
    w_uq = din("w_uq", [256, 768])
    kv_norm_w = din("kv_norm_w", [128])
    w_ukv = din("w_ukv", [128, 1024])
    conv_w = din("conv_w", [5, 1536])
    a_log_f = din("a_log_f", [1, 4])
    dt_bias_f = din("dt_bias_f", [1, 4])
    a_log_b = din("a_log_b", [1, 4])
    dt_bias_b = din("dt_bias_b", [1, 4])
    gdn_norm_w = din("gdn_norm_w", [1, 128])
    w_proj_a = din("w_proj_a", [512, D])
    w_proj_b = din("w_proj_b", [512, D])
    w_out = din("w_out", [D, D])
    norm2_w = din("norm2_w", [D])
    w_ff1 = din("w_ff1", [D, DFF])
    w_ff2 = din("w_ff2", [DFF, D])
    final_norm_w = din("final_norm_w", [1, D])
    cst = din("cst", [128, 128 * len(CONST_NAMES)])
    y = nc.dram_tensor("y", [T, D], F32, kind="ExternalOutput").ap()

    def dscr(name, shape, dt):
        return nc.dram_tensor(name, list(shape), dt, kind="Internal").ap()

    zq_s = dscr("zq_s", [12, 128, T + 4], BF16)
    sgdn_s = dscr("sgdn_s", [T, 512], BF16)
    sg_s = dscr("sg_s", [16, 128, T], BF16)
    oa_s = dscr("oa_s", [8, 64, T], BF16)
    x1_s = dscr("x1_s", [T, D], F32)
    ob_s = dscr("ob_s", [4, 128, T], BF16)
    gcd_s = dscr("gcd_s", [2, T // 128, 128], F32)
    wff1_s = dscr("wff1_s", [128, 8, DFF], BF16)
    wff2_s = dscr("wff2_s", [128, 32, D], BF16)

    dbg_out = {}

    def dbg_tensor(name, shape, dt=F32):
        dbg_out[name] = nc.dram_tensor("dbg_" + name, list(shape), dt, kind="ExternalOutput").ap()
        return dbg_out[name]

    P = Prog(nc)
    ctr = [0]

    def uname(s):
        ctr[0] += 1
        return "%s_%d" % (s, ctr[0])

    class Scope:
        def __init__(self):
            self.st = ExitStack()

        def sb(self, shape, dt, name="t"):
            return self.st.enter_context(nc.sbuf_tensor(uname(name), list(shape), dt))

        def ps(self, shape, dt=F32, name="p"):
            return self.st.enter_context(nc.psum_tensor(uname(name), list(shape), dt))

        def ring(self, n, shape, dt, name="r", psum=False):
            out = []
            for i in range(n):
                t = self.ps(shape, dt, name) if psum else self.sb(shape, dt, name)
                out.append((t, Buf(), P.new_tag()))
            return out

        def close(self):
            self.st.close()

    class Stop(Exception):
        pass

    def chk(name):
        if name in dbg:
            raise Stop()

    G = Scope()
    cst_t = G.sb([128, 128 * len(CONST_NAMES)], F32, "cst")
    b_cst = Buf()
    P.dma("sp", lambda e: e.dma_start(out=cst_t[:], in_=cst[:, :]), "cst", writes=[b_cst])

    def CF(name, rows=slice(0, 128), cols=slice(0, 128)):
        i = CONST_NAMES.index(name)
        c0 = i * 128 + cols.start
        c1 = i * 128 + cols.stop
        return cst_t[rows, c0:c1]

    cb_t = G.sb([128, 256], BF16, "cstb")
    b_cb = Buf()
    P.op("dve", lambda e: e.tensor_copy(out=cb_t[:], in_=cst_t[:, 0:256]), reads=[b_cst], writes=[b_cb])
    identb = cb_t[:, 0:128]
    onesb = cb_t[:, 128:256]
    bigb_t = G.sb([128, 256], BF16, "bigb")
    P.op("dve", lambda e: e.tensor_copy(out=bigb_t[:], in_=cst_t[:, 6 * 128:8 * 128]), reads=[b_cst], writes=[b_cb])

    gall = G.sb([128, NT, 16], F32, "gall")
    GA = Scope()
    cqnT = GA.sb([128, 2, T], BF16, "cqnT")
    ckvnT = GA.sb([128, T], BF16, "ckvnT")
    kropeT_full = GA.sb([96, T], BF16, "kropeT")
    kropeT = kropeT_full[64:96, :]
    cosT_full = GA.sb([96, T], BF16, "cosT")
    cosT = cosT_full[64:96, :]
    sinT_full = GA.sb([96, T], BF16, "sinT")
    sinT = sinT_full[64:96, :]
    b_cqn = [Buf() for _ in range(NB)]
    b_ckvn = [Buf() for _ in range(NB)]
    b_krope = [Buf() for _ in range(NB)]
    b_rope = Buf()
    b_gall = [Buf() for _ in range(NB)]

    alt = [0]

    def evac(out, in_, reads, writes, eng=None):
        if eng is None:
            alt[0] ^= 1
            eng = "act" if alt[0] else "dve"
        if eng == "act":
            P.op("act", lambda e: e.copy(out=out, in_=in_), reads=reads, writes=writes)
        elif eng == "dve":
            P.op("dve", lambda e: e.tensor_copy(out=out, in_=in_), reads=reads, writes=writes)
        else:
            P.op("pool", lambda e: e.tensor_copy(out=out, in_=in_), reads=reads, writes=writes)

    def scale_cast(out, in_, scale_ap, reads, writes, eng=None):
        if eng is None:
            alt[0] ^= 1
            eng = "act" if alt[0] else "dve"
        if scale_ap is None:
            return evac(out, in_, reads, writes, eng)
        if eng == "act":
            P.op("act", lambda e: e.activation(out=out, in_=in_, func=AF.Copy, scale=scale_ap), reads=reads, writes=writes)
        else:
            P.op("dve", lambda e: e.tensor_scalar(out=out, in0=in_, scalar1=scale_ap, scalar2=None, op0=ALU.mult),
                 reads=reads, writes=writes)

    def mm(out, lhsT, rhs, start, stop, reads, writes):
        P.op("pe", lambda e: e.matmul(out, lhsT=lhsT, rhs=rhs, start=start, stop=stop), reads=reads, writes=writes)

    def dbg_dump(name, src_ap, reads, shape, dt=F32):
        if name in dbg:
            d = dbg_tensor(name, shape, dt)
            P.dma("sp", lambda e: e.dma_start(out=d, in_=src_ap), uname("dbg"), reads=reads)

    def load_cols(scope, src, n, name):
        k = n // 128
        t = scope.sb([128, k], F32, name)
        b = Buf()
        P.dma("sp", lambda e: e.dma_start(out=t[:], in_=src.rearrange("(k p) -> p k", p=128), allow_slow_non_contiguous=True),
              P.new_tag(), writes=[b])
        return t, b

    TWO_PI = 2.0 * math.pi
    RC = min(T, 1024)

    def rope_gen(scope):
        pos_i = scope.sb([96, RC], I32, "pos_i")[64:96, :]
        u0 = scope.sb([96, RC], F32, "u0")[64:96, :]
        u1 = scope.sb([96, RC], F32, "u1")[64:96, :]
        ki = scope.sb([96, RC], I32, "ki")[64:96, :]
        b_pos, b_u0, b_u1, b_ki = Buf(), Buf(), Buf(), Buf()
        invf = CF("misc", slice(64, 96), slice(2, 3))
        tagp = P.new_tag()
        pf = pos_i.bitcast(F32)
        for c0 in range(0, T, RC):
            cs_ = slice(c0, c0 + RC)
            P.dma("sp", lambda e, cs_=cs_: e.dma_start(out=pos_i, in_=pos[0:1, cs_].to_broadcast([32, RC])), tagp, writes=[b_pos])
            P.op("dve", lambda e: e.tensor_copy(out=u0, in_=pos_i), reads=[b_pos], writes=[b_u0])
            P.op("dve", lambda e: e.tensor_scalar(out=u0, in0=u0, scalar1=invf, scalar2=1.0 / TWO_PI, op0=ALU.mult, op1=ALU.mult),
                 reads=[b_u0, b_cst], writes=[b_u0])
            yield
            for (dst, shift) in ((sinT, 0.0), (cosT, 0.25)):
                P.op("dve", lambda e, shift=shift: e.tensor_scalar(out=u1, in0=u0, scalar1=shift, scalar2=None, op0=ALU.add), reads=[b_u0], writes=[b_u1])
                P.op("dve", lambda e: e.tensor_copy(out=ki, in_=u1), reads=[b_u1], writes=[b_ki])
                yield
                P.op("dve", lambda e: e.tensor_copy(out=pf, in_=ki), reads=[b_ki], writes=[b_pos])
                P.op("dve", lambda e: e.tensor_tensor(out=u1, in0=u1, in1=pf, op=ALU.subtract), reads=[b_u1, b_pos], writes=[b_u1])
                yield
                P.op("dve", lambda e: e.tensor_scalar(out=pf, in0=u1, scalar1=0.5, scalar2=None, op0=ALU.is_gt), reads=[b_u1], writes=[b_pos])
                P.op("dve", lambda e: e.tensor_tensor(out=u1, in0=u1, in1=pf, op=ALU.subtract), reads=[b_u1, b_pos], writes=[b_u1])
                yield
                P.op("dve", lambda e: e.tensor_scalar(out=pf, in0=u1, scalar1=-0.5, scalar2=None, op0=ALU.is_lt), reads=[b_u1], writes=[b_pos])
                P.op("dve", lambda e: e.tensor_tensor(out=u1, in0=u1, in1=pf, op=ALU.add), reads=[b_u1, b_pos], writes=[b_u1])
                P.op("act", lambda e, dst=dst, cs_=cs_: e.activation(out=dst[:, cs_], in_=u1, func=AF.Sin, scale=TWO_PI * (1.0 - 2e-6)),
                     reads=[b_u1], writes=[b_rope])
                yield

    try:
        SA = Scope()
        xnT = SA.sb([128, 8, T], BF16, "xnT")
        b_xnT = [Buf() for _ in range(NB)]
        n1w = SA.sb([128, D], F32, "n1w")
        b_n1w = Buf()
        P.dma("sp", lambda e: e.dma_start(out=n1w[:], in_=norm1_w.rearrange("(o n) -> o n", o=1).to_broadcast([128, D])), P.new_tag(), writes=[b_n1w])
        xring = SA.ring(2, [128, D], F32, "xr")
        xnring = SA.ring(2, [128, D], BF16, "xn")
        junk = SA.sb([128, D], BF16, "junk")
        b_junk = Buf()
        ss = SA.sb([128, NT], F32, "ss")
        rs = SA.sb([128, NT], F32, "rs")
        b_ss = [Buf() for _ in range(NT)]
        pT = SA.ring(2, [128, 8, 128], BF16, "pT", psum=True)
        pz = SA.ring(2, [128, 512], F32, "pz", psum=True)
        pss_t = SA.ps([128, 512], F32, "pss")
        b_pss = Buf()
        pk_a = SA.ps([96, 512], F32, "pka")[64:96, :]
        pk_b = SA.ps([96, 512], F32, "pkb")[64:96, :]
        b_pk = Buf()
        wbr = SA.ring(3, [128, 8, 256], BF16, "wb")

        rg = [rope_gen(SA)]
        for t in range(NT):
            xt, bx, tagx = xring[t % 2]
            xn, bxn, _ = xnring[t % 2]
            pTt, bpT, _ = pT[t % 2]
            P.dma("sp", lambda e, xt=xt, t=t: e.dma_start(out=xt[:], in_=x[t * 128:(t + 1) * 128, :]), tagx, writes=[bx])
            P.op("act", lambda e, xt=xt, t=t: e.activation(out=junk[:], in_=xt[:], func=AF.Square, accum_out=ss[:, t:t + 1]),
                 reads=[bx], writes=[b_junk, b_ss[t]])
            P.op("act", lambda e, t=t: e.activation(out=rs[:, t:t + 1], in_=ss[:, t:t + 1], func=AF.Sqrt, scale=1.0 / D, bias=EPS),
                 reads=[b_ss[t]], writes=[b_ss[t]])
            P.op("dve", lambda e, t=t: e.reciprocal(out=rs[:, t:t + 1], in_=rs[:, t:t + 1]), reads=[b_ss[t]], writes=[b_ss[t]])
            P.op("dve", lambda e, xt=xt, xn=xn, t=t: e.scalar_tensor_tensor(out=xn[:], in0=xt[:], scalar=rs[:, t:t + 1], in1=n1w[:], op0=ALU.mult, op1=ALU.mult),
                 reads=[bx, b_ss[t], b_n1w], writes=[bxn])
            for k in range(8):
                P.op("pe", lambda e, k=k, xn=xn, pTt=pTt: e.transpose(out=pTt[:, k, :], in_=xn[:, k * 128:(k + 1) * 128], identity=identb),
                     reads=[bxn, b_cb], writes=[bpT])
            evac(xnT[:, :, t * 128:(t + 1) * 128], pTt[:], [bpT], [b_xnT[t // 4]])
            for _ in range(2):
                if rg[0] is not None:
                    try:
                        next(rg[0])
                    except StopIteration:
                        rg[0] = None
        if rg[0] is not None:
            for _ in rg[0]:
                pass
        dbg_dump("cosT", cosT, [b_rope], [32, T], BF16)
        dbg_dump("sinT", sinT, [b_rope], [32, T], BF16)

        chk("stopA1")
        gctr = [0]
        w_in_v = w_in.rearrange("(k p) n -> p k n", p=128)

        def load_group(c0, ncol):
            i = gctr[0]
            gctr[0] += 1
            wb, bwb, tag = wbr[i % 3]
            P.dma("pool", lambda e: e.dma_start(out=wb[:, :, 0:ncol], in_=w_in_v[:, :, c0:c0 + ncol]), tag, writes=[bwb])
            return wb, bwb

        def proj_fm(wb, bwb, c_lo, ncol, blk, ptile, bp):
            for k in range(8):
                mm(ptile, wb[:, k, c_lo:c_lo + ncol], xnT[:, k, blk * 512:(blk + 1) * 512], k == 0, k == 7,
                   [bwb, b_xnT[blk]], [bp])

        cqf = SA.sb([128, 2, 512], F32, "cqf")
        b_cqf_m = [Buf(), Buf()]
        sqb = SA.sb([128, 2, 512], BF16, "sqb")
        b_sqb_m = [Buf(), Buf()]
        rq = SA.sb([128, 512], F32, "rq")
        b_rq = Buf()
        zc = [0]

        def next_pz():
            zc[0] += 1
            return pz[zc[0] % 2]

        def norm_proj(c0, nchunk, dst_fn, dst_bufs):
            wb, bwb = load_group(c0, nchunk * 128)
            chk("stopA2a")
            for blk in range(NB):
                for m in range(nchunk):
                    pzt, bpz, _ = next_pz()
                    proj_fm(wb, bwb, m * 128, 128, blk, pzt[:], bpz)
                    chk("stopA2m")
                    P.op("act", lambda e, pzt=pzt, m=m: e.copy(out=cqf[:, m, :], in_=pzt[:]), reads=[bpz], writes=[b_cqf_m[m]])
                    P.op("pool", lambda e, m=m: e.tensor_tensor(out=sqb[:, m, :], in0=cqf[:, m, :], in1=cqf[:, m, :], op=ALU.mult),
                         reads=[b_cqf_m[m]], writes=[b_sqb_m[m]])
                    chk("stopA2n")
                chk("stopA2b_")
                for m in range(nchunk):
                    mm(pss_t[:], onesb, sqb[:, m, :], m == 0, m == nchunk - 1, [b_sqb_m[m], b_cb], [b_pss])
                chk("stopA2c")
                P.op("act", lambda e: e.activation(out=rq[:], in_=pss_t[:], func=AF.Sqrt, scale=1.0 / (nchunk * 128), bias=EPS),
                     reads=[b_pss], writes=[b_rq])
                chk("stopA2d")
                P.op("dve", lambda e: e.reciprocal(out=rq[:], in_=rq[:]), reads=[b_rq], writes=[b_rq])
                chk("stopA2e")
                for m in range(nchunk):
                    P.op("dve", lambda e, m=m, blk=blk: e.tensor_tensor(out=dst_fn(m, blk), in0=cqf[:, m, :], in1=rq[:], op=ALU.mult),
                         reads=[b_cqf_m[m], b_rq], writes=[dst_bufs[blk]])

        norm_proj(C_Q, 2, lambda m, blk: cqnT[:, m, blk * 512:(blk + 1) * 512], b_cqn)
        chk("stopA2f")
        norm_proj(C_KV, 1, lambda m, blk: ckvnT[:, blk * 512:(blk + 1) * 512], b_ckvn)
        dbg_dump("cqnT", cqnT[:], b_cqn, [128, 2, T], BF16)
        dbg_dump("ckvnT", ckvnT[:], b_ckvn, [128, T], BF16)

        chk("stopA2")
        wb, bwb = load_group(C_KR, 32)
        P.op("dve", lambda e, wb=wb: e.tensor_scalar(out=wb[:, :, 32:48], in0=wb[:, :, 16:32], scalar1=-1.0, scalar2=None, op0=ALU.mult),
             reads=[bwb], writes=[bwb])
        P.op("dve", lambda e, wb=wb: e.tensor_copy(out=wb[:, :, 48:64], in_=wb[:, :, 0:16]), reads=[bwb], writes=[bwb])
        t1 = SA.sb([96, 512], F32, "t1")[64:96, :]
        t2 = SA.sb([96, 512], F32, "t2")[64:96, :]
        b_t1, b_t2 = Buf(), Buf()
        for blk in range(NB):
            bs = slice(blk * 512, (blk + 1) * 512)
            proj_fm(wb, bwb, 0, 32, blk, pk_a, b_pk)
            proj_fm(wb, bwb, 32, 32, blk, pk_b, b_pk)
            P.op("dve", lambda e, bs=bs: e.tensor_tensor(out=t1, in0=pk_a, in1=cosT[:, bs], op=ALU.mult),
                 reads=[b_pk, b_rope], writes=[b_t1])
            P.op("dve", lambda e, bs=bs: e.tensor_tensor(out=t2, in0=pk_b, in1=sinT[:, bs], op=ALU.mult),
                 reads=[b_pk, b_rope], writes=[b_t2])
            P.op("dve", lambda e, bs=bs: e.tensor_tensor(out=kropeT[:, bs], in0=t1, in1=t2, op=ALU.add),
                 reads=[b_t1, b_t2], writes=[b_krope[blk]])
        dbg_dump("kropeT", kropeT, b_krope, [32, T], BF16)

        chk("stopA2b")
        dtb = SA.sb([128, 8], F32, "dtb")
        nega = SA.sb([128, 8], F32, "nega")
        b_dtb, b_nega = Buf(), Buf()
        for j, (src_dt, src_al) in enumerate(((dt_bias_f, a_log_f), (dt_bias_b, a_log_b))):
            P.dma("sp", lambda e, j=j, src_dt=src_dt: e.dma_start(out=dtb[:, j * 4:(j + 1) * 4], in_=src_dt[0:1, :].to_broadcast([128, 4])),
                  "gsm_dt", writes=[b_dtb])
            P.dma("sp", lambda e, j=j, src_al=src_al: e.dma_start(out=nega[:, j * 4:(j + 1) * 4], in_=src_al[0:1, :].to_broadcast([128, 4])),
                  "gsm_al", writes=[b_nega])
        P.op("act", lambda e: e.activation(out=nega[:], in_=nega[:], func=AF.Exp), reads=[b_nega], writes=[b_nega])
        P.op("dve", lambda e: e.tensor_scalar(out=nega[:], in0=nega[:], scalar1=-1.0, scalar2=None, op0=ALU.mult),
             reads=[b_nega], writes=[b_nega])
        wb, bwb = load_group(C_G, 16)
        gsb = SA.sb([128, 64], F32, "gsb")
        b_gsb = Buf()
        xg = SA.sb([128, 4, 8], F32, "xg")
        ax = SA.sb([128, 4, 8], F32, "ax")
        b_xg, b_ax = Buf(), Buf()
        for blk in range(NB):
            pzt, bpz, _ = next_pz()
            pv = pzt[:, 0:64].rearrange("p (i c) -> p i c", c=16)
            for i in range(4):
                tt = blk * 4 + i
                for k in range(8):
                    mm(pzt[:, i * 16:(i + 1) * 16], xnT[:, k, tt * 128:(tt + 1) * 128], wb[:, k, 0:16], k == 0, k == 7,
                       [bwb, b_xnT[blk]], [bpz])
            gsl = gall[:, blk * 4:(blk + 1) * 4, :]
            P.op("dve", lambda e, pzt=pzt: e.tensor_copy(out=gsb[:], in_=pzt[:, 0:64]), reads=[bpz], writes=[b_gsb])
            pv = gsb[:, :].rearrange("p (i c) -> p i c", c=16)
            bpz = b_gsb
            P.op("dve", lambda e, pv=pv: e.tensor_tensor(out=xg[:], in0=pv[:, :, 0:8], in1=dtb[:, :].unsqueeze(1).to_broadcast([128, 4, 8]), op=ALU.add),
                 reads=[bpz, b_dtb], writes=[b_xg])
            P.op("act", lambda e: e.activation(out=ax[:], in_=xg[:], func=AF.Abs), reads=[b_xg], writes=[b_ax])
            P.op("act", lambda e: e.activation(out=ax[:], in_=ax[:], func=AF.Exp, scale=-1.0), reads=[b_ax], writes=[b_ax])
            P.op("act", lambda e: e.activation(out=ax[:], in_=ax[:], func=AF.Ln, bias=1.0), reads=[b_ax], writes=[b_ax])
            P.op("dve", lambda e: e.tensor_single_scalar(out=xg[:], in_=xg[:], scalar=0.0, op=ALU.max), reads=[b_xg], writes=[b_xg])
            P.op("dve", lambda e: e.tensor_tensor(out=xg[:], in0=xg[:], in1=ax[:], op=ALU.add), reads=[b_xg, b_ax], writes=[b_xg])
            P.op("dve", lambda e, gsl=gsl: e.tensor_tensor(out=gsl[:, :, 0:8], in0=xg[:], in1=nega[:, :].unsqueeze(1).to_broadcast([128, 4, 8]), op=ALU.mult),
                 reads=[b_xg, b_nega], writes=[b_gall[blk]])
            P.op("act", lambda e, gsl=gsl, pv=pv: e.activation(out=gsl[:, :, 8:16], in_=pv[:, :, 8:16], func=AF.Sigmoid),
                 reads=[bpz], writes=[b_gall[blk]])
        dbg_dump("gall", gall[:], b_gall, [128, NT, 16], F32)

        chk("stopA3")
        og = SA.ring(2, [128, 4, 256], BF16, "og")
        ogc = [0]
        for half in range(2):
            wb, bwb = load_group(C_GG + half * 256, 256)
            for blk in range(NB):
                ot, bo, tago = og[ogc[0] % 2]
                ogc[0] += 1
                for i in range(4):
                    tt = blk * 4 + i
                    pzt, bpz, _ = next_pz()
                    for k in range(8):
                        mm(pzt[:, 0:256], xnT[:, k, tt * 128:(tt + 1) * 128], wb[:, k, 0:256], k == 0, k == 7, [bwb, b_xnT[blk]], [bpz])
                    P.op("act", lambda e, ot=ot, pzt=pzt, i=i: e.activation(out=ot[:, i, :], in_=pzt[:, 0:256], func=AF.Silu),
                         reads=[bpz], writes=[bo])
                P.dma("sp", lambda e, ot=ot, blk=blk, half=half: e.dma_start(
                    out=sgdn_s[blk * 512:(blk + 1) * 512, half * 256:(half + 1) * 256].rearrange("(i p) c -> p i c", p=128), in_=ot[:]),
                    tago, reads=[bo])
        ofm = SA.ring(2, [128, 512], BF16, "ofm")
        ofc = [0]

        def fm_group_to_dram(c0, func, dst_fn):
            wb, bwb = load_group(c0, 256)
            for m in range(2):
                for blk in range(NB):
                    pzt, bpz, _ = next_pz()
                    proj_fm(wb, bwb, m * 128, 128, blk, pzt[:], bpz)
                    ot, bo, tago = ofm[ofc[0] % 2]
                    ofc[0] += 1
                    if func is None:
                        evac(ot[:], pzt[:], [bpz], [bo])
                    else:
                        P.op("act", lambda e, ot=ot, pzt=pzt: e.activation(out=ot[:], in_=pzt[:], func=func), reads=[bpz], writes=[bo])
                    P.dma("sp", lambda e, ot=ot, m=m, blk=blk: e.dma_start(out=dst_fn(m, blk), in_=ot[:]), tago, reads=[bo])

        for g in range(8):
            fm_group_to_dram(C_GA + g * 256, AF.Sigmoid, lambda m, blk, g=g: sg_s[2 * g + m, :, blk * 512:(blk + 1) * 512])
        chk("stopA4")
        zpad = SA.sb([128, 12, 2], BF16, "zpad")
        b_zpad = Buf()
        P.op("pool", lambda e: e.memset(zpad[:], 0.0), writes=[b_zpad])
        P.dma("sp", lambda e: e.dma_start(out=zq_s[:, :, 0:2].rearrange("j p c -> p j c"), in_=zpad[:]), "zp", reads=[b_zpad])
        P.dma("sp", lambda e: e.dma_start(out=zq_s[:, :, T + 2:T + 4].rearrange("j p c -> p j c"), in_=zpad[:]), "zp", reads=[b_zpad])
        for g in range(6):
            fm_group_to_dram(C_QKV + g * 256, None, lambda m, blk, g=g: zq_s[2 * g + m, :, 2 + blk * 512:2 + (blk + 1) * 512])

        P.barrier()
        if "dramA" in dbg:
            for nm, src, shp in (("sgdn_s", sgdn_s, [T, 512]), ("sg_s", sg_s, [16, 128, T]), ("zq_s", zq_s, [12, 128, T + 4])):
                d = dbg_tensor(nm, shp, BF16)
                P.dma("sp", lambda e, d=d, src=src: e.dma_start(out=d, in_=src), uname("dbg"))
            P.barrier()
        P.emit()
        SA.close()
        chk("stopA")

        SC = Scope()
        qnc, b_qnc = load_cols(SC, q_norm_w, 256, "qnc")
        kvnc, b_kvnc = load_cols(SC, kv_norm_w, 128, "kvnc")
        stq = SC.sb([128, 2, 768], F32, "stq")
        stkv = SC.sb([128, 1024], F32, "stkv")
        b_stq, b_stkv = Buf(), Buf()
        P.dma("sp", lambda e: e.dma_start(out=stq[:], in_=w_uq.rearrange("(k p) n -> p k n", p=128)), P.new_tag(), writes=[b_stq])
        P.dma("sp", lambda e: e.dma_start(out=stkv[:], in_=w_ukv[:, :]), P.new_tag(), writes=[b_stkv])
        wqA = SC.sb([128, 2, 8, 96], BF16, "wqA")
        wqB = SC.sb([128, 2, 8, 32], BF16, "wqB")
        wkK = SC.sb([128, 8, 96], BF16, "wkK")
        wkV = SC.sb([128, 8, 64], BF16, "wkV")
        b_wq, b_wk = Buf(), Buf()
        for k in range(2):
            sv = stq[:, k, :].rearrange("p (h c) -> p h c", c=96)
            sc = qnc[:, k:k + 1]
            P.op("dve", lambda e, k=k, sv=sv, sc=sc: e.tensor_scalar(out=wqA[:, k, :, 64:96], in0=sv[:, :, 64:96], scalar1=sc, scalar2=None, op0=ALU.mult),
                 reads=[b_stq, b_qnc], writes=[b_wq])
            P.op("dve", lambda e, k=k, sv=sv, sc=sc: e.tensor_scalar(out=wqA[:, k, :, 0:64], in0=sv[:, :, 0:64], scalar1=sc, scalar2=None, op0=ALU.mult),
                 reads=[b_stq, b_qnc], writes=[b_wq])
            P.op("dve", lambda e, k=k, sv=sv, sc=sc: e.tensor_scalar(out=wqB[:, k, :, 0:16], in0=sv[:, :, 80:96], scalar1=sc, scalar2=-1.0, op0=ALU.mult, op1=ALU.mult),
                 reads=[b_stq, b_qnc], writes=[b_wq])
            P.op("dve", lambda e, k=k, sv=sv, sc=sc: e.tensor_scalar(out=wqB[:, k, :, 16:32], in0=sv[:, :, 64:80], scalar1=sc, scalar2=None, op0=ALU.mult),
                 reads=[b_stq, b_qnc], writes=[b_wq])
        kvv = stkv[:, :].rearrange("p (h c) -> p h c", c=128)
        P.op("pool", lambda e: e.memset(wkK[:], 0.0), writes=[b_wk])
        P.op("dve", lambda e: e.tensor_scalar(out=wkK[:, :, 0:64], in0=kvv[:, :, 0:64], scalar1=kvnc[:, 0:1], scalar2=None, op0=ALU.mult),
             reads=[b_stkv, b_kvnc, b_wk], writes=[b_wk])
        P.op("dve", lambda e: e.tensor_scalar(out=wkV[:], in0=kvv[:, :, 64:128], scalar1=kvnc[:, 0:1], scalar2=None, op0=ALU.mult),
             reads=[b_stkv, b_kvnc], writes=[b_wk])

        V_all = SC.sb([128, NT, 8, 64], BF16, "V_all")
        b_V = Buf()
        VAr = SC.ring(2, [128, NT, 128], BF16, "VA")
        for (va, bva, _) in VAr:
            P.op("pool", lambda e, va=va: e.memset(va[:], 1.0), writes=[bva])
        KTr = SC.ring(2, [96, T], BF16, "KT")
        QTr = SC.ring(2, [96, T], BF16, "QT")
        pTr = SC.ring(3, [128, 2, 512], BF16, "pTs")
        rDr = SC.ring(2, [128, 512], F32, "rD")
        onr = SC.ring(2, [64, 512], BF16, "on")
        tq1 = SC.sb([96, 256], F32, "tq1")[64:96, :]
        tq2 = SC.sb([96, 256], F32, "tq2")[64:96, :]
        b_tq1, b_tq2 = Buf(), Buf()
        ps_s = SC.ring(2, [128, 2, 512], F32, "ps_s", psum=True)
        po_r = SC.ring(2, [128, 512], F32, "po", psum=True)
        pkv = SC.ps([128, 512], F32, "pkv")
        b_pkv = Buf()
        pqa = SC.ps([96, 2, 256], F32, "pqa")
        b_pqa = Buf()
        for t in range(NT):
            mm(pkv[:], ckvnT[:, t * 128:(t + 1) * 128], wkV[:].rearrange("p h c -> p (h c)"), True, True, [b_ckvn[t // 4], b_wk], [b_pkv])
            P.op("act", lambda e, t=t: e.copy(out=V_all[:, t, :, :], in_=pkv[:].rearrange("p (h c) -> p h c", c=64)),
                 reads=[b_pkv], writes=[b_V])
        sm_scale = 1.0 / math.sqrt(96.0)

        def build_head(h):
            KT, bKT, _ = KTr[h % 2]
            QT, bQT, _ = QTr[h % 2]
            va, bva, _ = VAr[h % 2]
            P.op("dve", lambda e: e.tensor_copy(out=KT[64:96, :], in_=kropeT), reads=b_krope, writes=[bKT])
            P.op("dve", lambda e: e.tensor_copy(out=va[:, :, 0:64], in_=V_all[:, :, h, :]), reads=[b_V], writes=[bva])
            yield
            for blk in range(NB):
                bs = slice(blk * 512, (blk + 1) * 512)
                mm(pkv[0:96, :], wkK[:, h, :], ckvnT[:, bs], True, True, [b_wk, b_ckvn[blk]], [b_pkv])
                P.op("act", lambda e, bs=bs: e.copy(out=KT[0:64, bs], in_=pkv[0:64, :]), reads=[b_pkv], writes=[bKT])
                for hf in range(2):
                    hs = slice(blk * 512 + hf * 256, blk * 512 + (hf + 1) * 256)
                    for k in range(2):
                        mm(pqa[:, 0, :], wqA[:, k, h, :], cqnT[:, k, hs], k == 0, k == 1, [b_wq, b_cqn[blk]], [b_pqa])
                    for k in range(2):
                        mm(pqa[64:96, 1, :], wqB[:, k, h, :], cqnT[:, k, hs], k == 0, k == 1, [b_wq, b_cqn[blk]], [b_pqa])
                    P.op("dve", lambda e, hs=hs: e.tensor_copy(out=QT[0:64, hs], in_=pqa[0:64, 0, :]), reads=[b_pqa], writes=[bQT])
                    P.op("dve", lambda e, hs=hs: e.tensor_tensor(out=tq1, in0=pqa[64:96, 0, :], in1=cosT[:, hs], op=ALU.mult),
                         reads=[b_pqa, b_rope], writes=[b_tq1])
                    P.op("dve", lambda e, hs=hs: e.tensor_tensor(out=tq2, in0=pqa[64:96, 1, :], in1=sinT[:, hs], op=ALU.mult),
                         reads=[b_pqa, b_rope], writes=[b_tq2])
                    P.op("dve", lambda e, hs=hs: e.tensor_tensor(out=QT[64:96, hs], in0=tq1, in1=tq2, op=ALU.add),
                         reads=[b_tq1, b_tq2], writes=[bQT])
                    yield

        def run_all(g):
            for _ in g:
                pass

        run_all(build_head(0))
        if "QT0" in dbg:
            dbg_dump("QT0", QTr[0][0][:], [QTr[0][1]], [96, T], BF16)
            dbg_dump("KT0", KTr[0][0][:], [KTr[0][1]], [96, T], BF16)
        NP2 = NT // 2
        units = [(h, qb, kp) for h in range(8) for qb in range(NB) for kp in range(NP2)]
        occ = {}

        def emit_S(u):
            h, qb, kp = units[u]
            KT, bKT, _ = KTr[h % 2]
            QT, bQT, _ = QTr[h % 2]
            pss, bpss, _ = ps_s[u % 2]
            for j in range(2):
                kt = 2 * kp + j
                mm(pss[:, j, :], KT[:, kt * 128:(kt + 1) * 128], QT[:, qb * 512:(qb + 1) * 512], True, True, [bKT, bQT], [bpss])

        def emit_exp(u):
            pss, bpss, _ = ps_s[u % 2]
            pT_, bpT_, _ = pTr[u % 3]
            P.op("act", lambda e: e.activation(out=pT_[:], in_=pss[:], func=AF.Exp, scale=sm_scale), reads=[bpss], writes=[bpT_])

        def emit_PV(u):
            h, qb, kp = units[u]
            va, bva, _ = VAr[h % 2]
            pT_, bpT_, _ = pTr[u % 3]
            oi = (h * NB + qb)
            po, bpo, _ = po_r[oi % 2]
            for j in range(2):
                kt = 2 * kp + j
                mm(po[:], va[:, kt, :], pT_[:, j, :], kt == 0, kt == NT - 1, [bva, bpT_], [bpo])
            if kp == NP2 - 1:
                rD, brD, _ = rDr[oi % 2]
                on, bon, tagon = onr[oi % 2]
                P.op("dve", lambda e: e.reciprocal(out=rD[64:128, :], in_=po[64:128, :]), reads=[bpo], writes=[brD])
                P.op("dve", lambda e: e.tensor_tensor(out=on[:], in0=po[0:64, :], in1=rD[64:128, :], op=ALU.mult), reads=[bpo, brD], writes=[bon])
                P.dma("sp", lambda e: e.dma_start(out=oa_s[h, :, qb * 512:(qb + 1) * 512], in_=on[:]), tagon, reads=[bon])

        bgen = None
        emit_S(0)
        for u in range(len(units)):
            h, qb, kp = units[u]
            if qb == 0 and kp == 0 and h + 1 < 8:
                bgen = build_head(h + 1)
            emit_exp(u)
            if u + 1 < len(units):
                h2_, _, _ = units[u + 1]
                if h2_ != h and bgen is not None:
                    run_all(bgen)
                    bgen = None
                emit_S(u + 1)
            emit_PV(u)
            if bgen is not None and kp % 4 == 3:
                try:
                    next(bgen)
                except StopIteration:
                    bgen = None
        P.barrier()
        if "oa_s" in dbg:
            d = dbg_tensor("oa_s", [8, 64, T], BF16)
            P.dma("sp", lambda e: e.dma_start(out=d, in_=oa_s), uname("dbg"))
            P.barrier()
        P.emit()
        SC.close()
        GA.close()
        chk("stopC1")


        SB = Scope()
        obr = SB.ring(2, [128, 128], BF16, "obr")
        obc = [0]
        cwT = SB.sb([128, 12, 5], F32, "cwT")
        b_cwT = Buf()
        tagc = P.new_tag()
        for tap in range(5):
            P.dma("sp", lambda e, tap=tap: e.dma_start(out=cwT[:, :, tap], in_=conv_w[tap].rearrange("(ch c) -> c ch", c=128),
                                                       allow_slow_non_contiguous=True), tagc, writes=[b_cwT])
        gnw = SB.sb([128, 128], F32, "gnw")
        b_gnw = Buf()
        P.dma("sp", lambda e: e.dma_start(out=gnw[:], in_=gdn_norm_w[0:1, :].to_broadcast([128, 128])), P.new_tag(), writes=[b_gnw])
        zcr = SB.ring(2, [128, T + 4], BF16, "zc")
        dgr = SB.ring(2, [128, 5, 128], BF16, "dg")
        qn = SB.sb([128, NT, 128], BF16, "qn")
        kn = SB.sb([128, NT, 128], BF16, "kn")
        vt = SB.sb([128, NT, 128], BF16, "vt")
        b_qkv = [[Buf() for _ in range(NT)] for _ in range(3)]
        o_f = SB.sb([128, NT, 128], BF16, "o_f")
        b_of = [Buf() for _ in range(NT)]
        o_w = SB.sb([128, NT, 128], BF16, "o_w")
        b_ow = [Buf() for _ in range(NT)]
        sgh = SB.sb([128, NT, 128], BF16, "sgh")
        b_sgh = Buf()
        s4r = SB.ring(2, [128, 4, 128], F32, "s4")
        sq4 = SB.sb([128, 4, 128], F32, "sq4")
        b_sq4 = Buf()
        ss4 = SB.sb([128, 4], F32, "ss4")
        b_ss4 = Buf()
        pbank = {d_: [SB.ps([128, 512], F32, "pbk") for _ in range(4)] for d_ in range(2)}
        bbank = {d_: [Buf(excl=True) for _ in range(4)] for d_ in range(2)}
        pcr = [(pbank[d_][0][:, :].rearrange("p (a c) -> p a c", c=128), bbank[d_][0], None) for d_ in range(2)]
        identf = CF("ident")
        onesf = CF("ones")

        def conv_chunk(j, kind, dst, dst_bufs, ci):
            zc, bzc, tagz = zcr[ci % 2]
            dg, bdg, _ = dgr[ci % 2]
            P.dma("sp", lambda e: e.dma_start(out=zc[:], in_=zq_s[j]), tagz, writes=[bzc])
            for tap in range(5):
                P.op("pool", lambda e, tap=tap: e.tensor_scalar(out=dg[:, tap, :], in0=identf, scalar1=cwT[:, j, tap:tap + 1], scalar2=None, op0=ALU.mult),
                     reads=[b_cwT, b_cst], writes=[bdg])
            for blk in range(NB):
                pc, bpc, _ = pcr[blk % 2]
                s4, bs4, _ = s4r[blk % 2]
                for i in range(4):
                    tt = blk * 4 + i
                    for tap in range(5):
                        mm(pc[:, i, :], zc[:, tt * 128 + tap: tt * 128 + tap + 128], dg[:, tap, :], tap == 0, tap == 4, [bzc, bdg], [bpc])
                P.op("act", lambda e, pc=pc, s4=s4: e.activation(out=s4[:], in_=pc[:], func=AF.Silu), reads=[bpc], writes=[bs4])
                if kind == "v":
                    P.op("pool", lambda e, s4=s4, blk=blk: e.tensor_copy(out=dst[:, blk * 4:(blk + 1) * 4, :], in_=s4[:]),
                         reads=[bs4], writes=[dst_bufs[blk * 4 + i] for i in range(4)])
                else:
                    for i in range(4):
                        P.op("act", lambda e, s4=s4, i=i: e.activation(out=sq4[:, i, :], in_=s4[:, i, :], func=AF.Square, accum_out=ss4[:, i:i + 1]),
                             reads=[bs4], writes=[b_sq4, b_ss4])
                    P.op("act", lambda e: e.activation(out=ss4[:], in_=ss4[:], func=AF.Sqrt, bias=EPS), reads=[b_ss4], writes=[b_ss4])
                    P.op("dve", lambda e: e.reciprocal(out=ss4[:], in_=ss4[:]), reads=[b_ss4], writes=[b_ss4])
                    for i in range(4):
                        tt = blk * 4 + i
                        sc2 = (128.0 ** -0.5) if kind == "q" else 1.0
                        P.op("dve", lambda e, s4=s4, i=i, tt=tt, sc2=sc2: e.tensor_scalar(out=dst[:, tt, :], in0=s4[:, i, :], scalar1=ss4[:, i:i + 1],
                                                                                         scalar2=sc2, op0=ALU.mult, op1=ALU.mult),
                             reads=[bs4, b_ss4], writes=[dst_bufs[tt]])

        RD = 4

        def mk(shape, dt, name):
            return [[SB.sb(shape, dt, name), Buf()] for _ in range(RD)]

        IB = {}
        for d_ in range(2):
            IB[d_] = dict(
                kqT=mk([128, 2, 128], BF16, "kqT"), gv=mk([128, 4], F32, "gv"), gcs=mk([128, 6], F32, "gcs"),
                exi=mk([128, 4], F32, "exi"), ex=mk([128, 4], F32, "ex"), dgc=mk([128, 128], F32, "dgc"),
                dec=mk([128, 128], F32, "dec"), Lq=mk([128, 2, 128], BF16, "Lq"),
                LqT=mk([128, 2, 128], BF16, "LqT"), yb=mk([128, 256], BF16, "yb"), yb2=mk([128, 256], BF16, "yb2"),
                W=[mk([128, 2, 128], BF16, "W") for _ in range(2)], WpI=[mk([128, 128], BF16, "WpI") for _ in range(2)], bk=mk([128, 1], F32, "bk"),
                qg=mk([128, 128], BF16, "qg"), kdec=mk([128, 2, 128], BF16, "kdec"),
                QnT=mk([128, 128], BF16, "QnT"), O0=mk([128, 128], F32, "O0"), MT=mk([128, 2, 128], BF16, "MT"), Cc=mk([128, 2, 128], F32, "Cc"),
                Sbs=[[SB.sb([128, 128], BF16, "Sb"), Buf()] for _ in range(2)],
            )
            pb, bb = pbank[d_], bbank[d_]
            IB[d_].update(
                p_B1=[pb[0][:, 0:128], bb[0]], p_gc=[pb[0][:, 128:136], bb[0]],
                p_gq=[pb[1][:, 0:256].rearrange("p (a c) -> p a c", c=128), bb[1]],
                p_kq=[pb[1][:, 256:384].bitcast(BF16).rearrange("p (a c) -> p a c", c=128), bb[1]],
                p_tr=[pb[1][:, 384:512].bitcast(BF16).rearrange("p (a c) -> p a c", c=128), bb[1]],
                p_w=[[pb[2][:, 0:256].rearrange("p (a c) -> p a c", c=128), bb[2]], [pb[0][:, 0:256].rearrange("p (a c) -> p a c", c=128), bb[0]]],
                p_y=[[pb[2][:, 256:512], bb[2]], [pb[0][:, 256:512], bb[0]]],
                p_a=[pb[3][:, 0:128], bb[3]], p_s=[pb[3][:, 128:256], bb[3]],
            )

        kqT_all = SB.sb([128, NT, 2, 128], BF16, "kqT_all")
        b_kqTa = [Buf() for _ in range(NT)]
        PRO = {}
        for d_ in range(2):
            PRO[d_] = (SB.sb([128, 3, NT, 2], F32, "gcs_a"), SB.sb([128, NT, 4], F32, "ex_a"), SB.sb([128, NT], F32, "bk_a"), Buf())
        nb_all = {d_: SB.sb([128, NT], F32, "nb_a") for d_ in range(2)}
        nex_all = {d_: SB.sb([128, NT], F32, "nex_a") for d_ in range(2)}
        exm_all = {d_: SB.sb([128, NT, 2], F32, "exm_a") for d_ in range(2)}
        negidb = SB.sb([128, 128], BF16, "negidb")
        P.op("dve", lambda e: e.tensor_scalar(out=negidb[:], in0=identf, scalar1=-1.0, scalar2=None, op0=ALU.mult), reads=[b_cst], writes=[b_cb])
        gcT_b = {d_: (SB.sb([NT, 128], F32, "gcT"), Buf()) for d_ in range(2)}
        GR_b = {d_: (SB.sb([128, NT, 128], F32, "GR"), Buf()) for d_ in range(2)}
        b_gcd = {d_: Buf() for d_ in range(2)}
        tag_gc = {d_: (P.new_tag(), P.new_tag()) for d_ in range(2)}
        gv_a = SB.sb([128, NT, 4], F32, "gv_a")
        exi_a = SB.sb([128, NT, 4], F32, "exi_a")
        b_gva, b_exia = Buf(), Buf()

        def prologue(h, d_):
            gcs_a, ex_a, bk_a, b_pro = PRO[d_]
            g_all = gall[:, :, d_ * 4 + h]
            beta_all = gall[:, :, 8 + d_ * 4 + h]
            P.op("dve", lambda e: e.tensor_copy(out=gv_a[:, :, 0], in_=g_all), reads=b_gall, writes=[b_gva])
            P.op("dve", lambda e: e.tensor_copy(out=gv_a[:, :, 1], in_=g_all), reads=b_gall, writes=[b_gva])
            P.op("dve", lambda e: e.tensor_scalar(out=gv_a[:, :, 2], in0=g_all, scalar1=CF("misc", cols=slice(0, 1)), scalar2=None, op0=ALU.mult),
                 reads=b_gall + [b_cst], writes=[b_gva])
            P.op("dve", lambda e: e.tensor_scalar(out=gv_a[:, :, 3], in0=g_all, scalar1=CF("misc", cols=slice(1, 2)), scalar2=None, op0=ALU.mult),
                 reads=b_gall + [b_cst], writes=[b_gva])
            pb, bb = pbank[d_][0], bbank[d_][0]
            mcum = CF("MUi") if d_ == 0 else CF("MLi")
            mm(pb[:, 0:2 * NT].rearrange("p (t c) -> p t c", c=2), mcum, gv_a[:, :, 0:2], True, True, [b_gva, b_cst], [bb])
            mm(pb[:, 2 * NT:4 * NT].rearrange("p (t c) -> p t c", c=2), CF("BD"), gv_a[:, :, 0:2], True, True, [b_gva, b_cst], [bb])
            mm(pb[:, 4 * NT:6 * NT].rearrange("p (t c) -> p t c", c=2), onesf, gv_a[:, :, 2:4], True, True, [b_gva, b_cst], [bb])
            P.op("act", lambda e: e.copy(out=gcs_a[:], in_=pb[:, 0:6 * NT].rearrange("p (a t c) -> p a t c", a=3, c=2)), reads=[bb], writes=[b_pro])
            P.op("dve", lambda e: e.tensor_copy(out=exi_a[:, :, 0], in_=gcs_a[:, 0, :, 0]), reads=[b_pro], writes=[b_exia])
            P.op("dve", lambda e: e.tensor_tensor(out=exi_a[:, :, 1], in0=gcs_a[:, 1, :, 0], in1=gcs_a[:, 0, :, 0], op=ALU.subtract), reads=[b_pro], writes=[b_exia])
            P.op("dve", lambda e: e.tensor_copy(out=exi_a[:, :, 2:4], in_=gcs_a[:, 2, :, :]), reads=[b_pro], writes=[b_exia])
            P.op("act", lambda e: e.activation(out=ex_a[:], in_=exi_a[:], func=AF.Exp), reads=[b_exia], writes=[b_pro])
            P.op("dve", lambda e: e.tensor_tensor(out=bk_a[:], in0=beta_all, in1=ex_a[:, :, 0], op=ALU.mult), reads=b_gall + [b_pro], writes=[b_pro])
            P.op("dve", lambda e: e.tensor_scalar(out=nb_all[d_][:], in0=beta_all, scalar1=-1.0, scalar2=None, op0=ALU.mult), reads=b_gall + [b_pro], writes=[b_pro])
            P.op("dve", lambda e: e.tensor_scalar(out=nex_all[d_][:], in0=ex_a[:, :, 0], scalar1=-1.0, scalar2=None, op0=ALU.mult), reads=[b_pro], writes=[b_pro])
            for c in range(2):
                P.op("dve", lambda e, c=c: e.tensor_scalar(out=exm_all[d_][:, :, c], in0=ex_a[:, :, 1], scalar1=CF("misc", cols=slice(c, c + 1)), scalar2=None,
                                                           op0=ALU.mult), reads=[b_pro, b_cst], writes=[b_pro])
            gcT, b_gcT = gcT_b[d_]
            GR, b_GR = GR_b[d_]
            P.op("pe", lambda e: e.transpose(out=pb[0:NT, 0:128], in_=gcs_a[:, 0, :, 0], identity=identf), reads=[b_pro, b_cst], writes=[bb])
            P.op("act", lambda e: e.copy(out=gcT[:], in_=pb[0:NT, 0:128]), reads=[bb], writes=[b_gcT])
            P.dma("sp", lambda e: e.dma_start(out=gcd_s[d_], in_=gcT[:]), tag_gc[d_][0], reads=[b_gcT], writes=[b_gcd[d_]])
            P.dma("sp", lambda e: e.dma_start(out=GR[:].rearrange("p t c -> p (t c)"),
                                              in_=gcd_s[d_:d_ + 1].rearrange("o t c -> o (t c)").to_broadcast([128, NT * 128])),
                  tag_gc[d_][1], reads=[b_gcd[d_]], writes=[b_GR])
            P.op("dve", lambda e: e.tensor_tensor(out=GR[:], in0=GR[:], in1=(CF("BIGf") if d_ == 0 else CF("BIGb")).unsqueeze(1).to_broadcast([128, NT, 128]),
                                                  op=ALU.add), reads=[b_GR, b_cst], writes=[b_GR])

        def solve(h, d_, t, slot):
            B_ = IB[d_]
            g = gall[:, t, d_ * 4 + h: d_ * 4 + h + 1]
            beta = gall[:, t, 8 + d_ * 4 + h: 8 + d_ * 4 + h + 1]
            bg = b_gall[t // 4]
            kqT = kqT_all[:, t, :, :]
            b_kqT = b_kqTa[t]
            gcs_a, ex_a, bk_a, b_pro = PRO[d_]
            gc_ap = gcs_a[:, 0, t, 0:1]
            ex = ex_a[:, t, :]
            b_ex = b_pro
            b_gcs = b_pro
            GR, b_GR = GR_b[d_]
            dec, b_dec = B_["dec"][slot]
            P.op("act", lambda e: e.activation(out=dec[:], in_=GR[:, t, :], func=AF.Exp, scale=-1.0, bias=gc_ap), reads=[b_GR, b_gcs], writes=[b_dec])
            yield
            p_gq, b_pgq = B_["p_gq"]
            mm(p_gq[:, 0, :], kqT[:, 0, :], kqT[:, 0, :], True, False, [b_kqT], [b_pgq])
            mm(p_gq[:, 0, :], negidb[:], identb, False, True, [b_cb], [b_pgq])
            mm(p_gq[:, 1, :], kqT[:, 1, :], kqT[:, 0, :], True, True, [b_kqT], [b_pgq])
            Lq, b_Lq = B_["Lq"][slot]
            nbeta = nb_all[d_][:, t:t + 1]
            P.op("dve", lambda e: e.scalar_tensor_tensor(out=Lq[:, 0, :], in0=p_gq[:, 0, :], scalar=nbeta, in1=dec[:], op0=ALU.mult, op1=ALU.mult),
                 reads=[b_pgq, b_pro, b_dec], writes=[b_Lq])
            P.op("dve", lambda e: e.tensor_tensor(out=Lq[:, 1, :], in0=p_gq[:, 1, :], in1=dec[:], op=ALU.mult), reads=[b_pgq, b_dec], writes=[b_Lq])
            yield
            p_tr, b_ptr = B_["p_tr"]
            LqT, b_LqT = B_["LqT"][slot]
            P.op("pe", lambda e: e.transpose(out=p_tr[:, 0, :], in_=Lq[:, 0, :], identity=identb), reads=[b_Lq, b_cb], writes=[b_ptr])
            P.op("pe", lambda e: e.transpose(out=p_tr[:, 1, :], in_=Lq[:, 1, :], identity=identb), reads=[b_Lq, b_cb], writes=[b_ptr])
            P.op("dve", lambda e: e.tensor_copy(out=LqT[:], in_=p_tr), reads=[b_ptr], writes=[b_LqT])
            WpI, b_WpI = B_["WpI"][0][slot]
            P.op("dve", lambda e, WpI=WpI: e.tensor_tensor(out=WpI[:], in0=LqT[:, 0, :], in1=identb, op=ALU.add), reads=[b_LqT, b_cb], writes=[b_WpI])
            ybs = [B_["yb"][slot], B_["yb2"][slot]]
            yb, b_yb = ybs[0]
            bk_ap = bk_a[:, t:t + 1]
            P.op("act", lambda e, yb=yb: e.activation(out=yb[:, 0:128], in_=vt[:, t, :], func=AF.Copy, scale=beta), reads=[b_qkv[2][t], bg], writes=[b_yb])
            P.op("act", lambda e, yb=yb: e.activation(out=yb[:, 128:256], in_=kn[:, t, :], func=AF.Copy, scale=bk_ap), reads=[b_qkv[1][t], b_pro], writes=[b_yb])
            yield
            p_y, b_py = B_["p_y"][slot % 2]
            p_w, b_pw = B_["p_w"][slot % 2]
            Wc, WcT, b_Wc = Lq[:, 0, :], LqT[:, 0, :], [b_Lq, b_LqT]
            for lev in range(6):
                yn, b_yn = ybs[(lev + 1) % 2]
                mm(p_y, WpI[:], yb[:], True, True, [b_WpI, b_yb], [b_py])
                if lev < 5:
                    Wn, b_Wn = B_["W"][lev % 2][slot]
                    WpIn, b_WpIn = B_["WpI"][(lev + 1) % 2][slot]
                    if lev < 4:
                        mm(p_w[:, 0, :], WcT, Wc, True, True, b_Wc, [b_pw])
                    mm(p_w[:, 1, :], Wc, WcT, True, True, b_Wc, [b_pw])
                P.op("act", lambda e, yn=yn: e.copy(out=yn[:], in_=p_y), reads=[b_py], writes=[b_yn])
                if lev < 5:
                    P.op("dve", lambda e, WpIn=WpIn: e.tensor_tensor(out=WpIn[:], in0=p_w[:, 1, :], in1=identf, op=ALU.add),
                         reads=[b_pw, b_cst], writes=[b_WpIn])
                    if lev < 4:
                        P.op("dve", lambda e, Wn=Wn: e.tensor_copy(out=Wn[:], in_=p_w), reads=[b_pw], writes=[b_Wn])
                    Wc, WcT, b_Wc = Wn[:, 0, :], Wn[:, 1, :], [b_Wn]
                    WpI, b_WpI = WpIn, b_WpIn
                yb, b_yb = yn, b_yn
                yield
            sol, b_sol = B_["yb"][slot]
            nqg, b_nqg = B_["qg"][slot]
            kdec, b_kdec = B_["kdec"][slot]
            QnT, b_QnT = B_["QnT"][slot]
            O0, b_O0 = B_["O0"][slot]
            MT, b_MT = B_["MT"][slot]
            Cc, b_Cc = B_["Cc"][slot]
            nex0 = nex_all[d_][:, t:t + 1]
            P.op("act", lambda e: e.activation(out=nqg[:], in_=qn[:, t, :], func=AF.Copy, scale=nex0), reads=[b_qkv[0][t], b_pro], writes=[b_nqg])
            for c in range(2):
                P.op("act", lambda e, c=c: e.activation(out=kdec[:, c, :], in_=kn[:, t, :], func=AF.Copy, scale=exm_all[d_][:, t, c:c + 1]),
                     reads=[b_qkv[1][t], b_pro], writes=[b_kdec])
            if "noA" in dbg:
                return
            mm(p_w[:, 0, :], sol[:, 128:256], LqT[:, 1, :], True, False, [b_sol, b_LqT], [b_pw])
            mm(p_w[:, 0, :], nqg[:], identb, False, True, [b_nqg, b_cb], [b_pw])
            mm(p_w[:, 1, :], LqT[:, 1, :], sol[:, 0:128], True, True, [b_sol, b_LqT], [b_pw])
            P.op("dve", lambda e: e.tensor_copy(out=QnT[:], in_=p_w[:, 0, :]), reads=[b_pw], writes=[b_QnT])
            P.op("act", lambda e: e.copy(out=O0[:], in_=p_w[:, 1, :]), reads=[b_pw], writes=[b_O0])
            yield
            if "noB" in dbg:
                return
            for c in range(2):
                cs = slice(c * 64, (c + 1) * 64)
                mm(p_w[:, c, :], sol[:, 128:256], kdec[:, c, :], True, True, [b_sol, b_kdec], [b_pw])
            for c in range(2):
                P.op("dve", lambda e, c=c: e.scalar_tensor_tensor(out=MT[:, c, :], in0=identf, scalar=ex[:, 2 + c:3 + c], in1=p_w[:, c, :],
                                                                  op0=ALU.mult, op1=ALU.subtract), reads=[b_pw, b_ex, b_cst], writes=[b_MT])
            yield
            if "noC" in dbg:
                return
            for c in range(2):
                cs = slice(c * 64, (c + 1) * 64)
                mm(p_w[:, c, :], kdec[:, c, :], sol[:, 0:128], True, True, [b_sol, b_kdec], [b_pw])
            P.op("act", lambda e: e.copy(out=Cc[:], in_=p_w), reads=[b_pw], writes=[b_Cc])
            yield

        def scan(h, d_, t, slot):
            B_ = IB[d_]
            QnT, b_QnT = B_["QnT"][slot]
            O0, b_O0 = B_["O0"][slot]
            MT, b_MT = B_["MT"][slot]
            Cc, b_Cc = B_["Cc"][slot]
            p_s, b_ps = B_["p_s"]
            p_o, b_po = B_["p_a"]
            if "noScan" in dbg:
                return
                yield
            for c in ((0, 1) if d_ == 0 else (1, 0)):
                cs = slice(c * 64, (c + 1) * 64)
                Sb, b_Sb = B_["Sbs"][Sidx[d_] % 2]
                Sn, b_Sn = B_["Sbs"][(Sidx[d_] + 1) % 2]
                Sidx[d_] += 1
                mm(p_s, MT[:, c, :], Sb[:], True, True, [b_MT, b_Sb], [b_ps])
                P.op("dve", lambda e, c=c, Sn=Sn: e.tensor_tensor(out=Sn[:], in0=p_s, in1=Cc[:, c, :], op=ALU.add), reads=[b_ps, b_Cc], writes=[b_Sn])
                yield
                dst, bdst = (o_f, b_of) if d_ == 0 else (o_w, b_ow)
                if "noPo" not in dbg:
                    mm(p_o[cs, :], QnT[:, cs], Sb[:], True, True, [b_QnT, b_Sb], [b_po])
                    P.op("dve", lambda e, cs=cs, dst=dst: e.tensor_tensor(out=dst[cs, t, :], in0=O0[cs, :], in1=p_o[cs, :], op=ALU.subtract),
                         reads=[b_po, b_O0], writes=[bdst[t]])
                else:
                    P.op("dve", lambda e, cs=cs, dst=dst: e.tensor_copy(out=dst[cs, t, :], in_=O0[cs, :]), reads=[b_O0], writes=[bdst[t]])
                yield

        Sidx = {0: 0, 1: 0}

        fin_bufs = [dict(osum=[SB.sb([128, 128], F32, "osum"), Buf()], ob=[SB.sb([128, 128], BF16, "ob"), Buf()],
                         ssn=[SB.sb([128, 1], F32, "ssn"), Buf()], junk=[SB.sb([128, 128], BF16, "junkb"), Buf()]) for _ in range(4)]

        def final(h, t, fb):
            osum, b_osum = fb["osum"]
            ob, b_ob = fb["ob"]
            ssn, b_ssn = fb["ssn"]
            junkb, b_junkb = fb["junk"]
            p_tr, b_ptr = IB[t % 2]["p_tr"]
            P.op("pool", lambda e: e.tensor_tensor(out=osum[:], in0=o_f[:, t, :], in1=o_w[:, t, :], op=ALU.add), reads=[b_of[t], b_ow[t]], writes=[b_osum])
            yield
            P.op("act", lambda e: e.activation(out=junkb[:], in_=osum[:], func=AF.Square, accum_out=ssn[:]), reads=[b_osum], writes=[b_junkb, b_ssn])
            P.op("act", lambda e: e.activation(out=ssn[:], in_=ssn[:], func=AF.Ln, scale=1.0 / 128, bias=EPS), reads=[b_ssn], writes=[b_ssn])
            P.op("act", lambda e: e.activation(out=ssn[:], in_=ssn[:], func=AF.Exp, scale=-0.5), reads=[b_ssn], writes=[b_ssn])
            yield
            P.op("dve", lambda e: e.scalar_tensor_tensor(out=osum[:], in0=osum[:], scalar=ssn[:, 0:1], in1=gnw[:], op0=ALU.mult, op1=ALU.mult),
                 reads=[b_osum, b_ssn, b_gnw], writes=[b_osum])
            yield
            P.op("pool", lambda e: e.tensor_tensor(out=ob[:], in0=osum[:], in1=sgh[:, t, :], op=ALU.mult), reads=[b_osum, b_sgh], writes=[b_ob])
            yield
            P.op("pe", lambda e: e.transpose(out=p_tr[:, 0, :], in_=ob[:], identity=identb), reads=[b_ob, b_cb], writes=[b_ptr])
            obt, b_obt, tago = obr[obc[0] % 2]
            obc[0] += 1
            P.op("dve", lambda e: e.tensor_copy(out=obt[:], in_=p_tr[:, 0, :]), reads=[b_ptr], writes=[b_obt])
            P.dma("sp", lambda e: e.dma_start(out=ob_s[h, :, t * 128:(t + 1) * 128], in_=obt[:]), tago, reads=[b_obt])
            yield

        IBo = {0: IB[0]["p_a"], 1: IB[1]["p_a"]}

        def run_rr(gens):
            gens = list(gens)
            while gens:
                for g_ in list(gens):
                    try:
                        next(g_)
                    except StopIteration:
                        gens.remove(g_)

        ci = 0
        for h in range(4):
            for d_ in range(2):
                prologue(h, d_)
            conv_chunk(h, "q", qn, b_qkv[0], ci); ci += 1
            conv_chunk(4 + h, "k", kn, b_qkv[1], ci); ci += 1
            conv_chunk(8 + h, "v", vt, b_qkv[2], ci); ci += 1
            P.dma("sp", lambda e, h=h: e.dma_start(out=sgh[:], in_=sgdn_s[:, h * 128:(h + 1) * 128].rearrange("(t p) c -> p t c", p=128)),
                  "sgh", writes=[b_sgh])
            if h == 0:
                dbg_dump("qn", qn[:], b_qkv[0], [128, NT, 128], BF16)
                dbg_dump("kn", kn[:], b_qkv[1], [128, NT, 128], BF16)
                dbg_dump("vt", vt[:], b_qkv[2], [128, NT, 128], BF16)
                chk("stopB1")
            for t in range(NT):
                p_kq, b_pkq = IB[t % 2]["p_kq"]
                P.op("pe", lambda e, t=t, p_kq=p_kq: e.transpose(out=p_kq[:, 0, :], in_=kn[:, t, :], identity=identb), reads=[b_qkv[1][t], b_cb], writes=[b_pkq])
                P.op("pe", lambda e, t=t, p_kq=p_kq: e.transpose(out=p_kq[:, 1, :], in_=qn[:, t, :], identity=identb), reads=[b_qkv[0][t], b_cb], writes=[b_pkq])
                P.op("dve", lambda e, t=t, p_kq=p_kq: e.tensor_copy(out=kqT_all[:, t, :, :], in_=p_kq), reads=[b_pkq], writes=[b_kqTa[t]])
            for d_ in range(2):
                for (Sb_, b_Sb_) in IB[d_]["Sbs"]:
                    P.op("pool", lambda e, Sb_=Sb_: e.memset(Sb_[:], 0.0), writes=[b_Sb_])
            NFL = ([int(x[3:]) for x in dbg if x.startswith("nfl")] or [3])[0]
            tasks = []
            orders = {0: list(range(NT)), 1: list(range(NT - 1, -1, -1))}
            for i in range(NT):
                for d_ in range(2):
                    tasks.append((("solve", d_, i), (lambda d_=d_, i=i: solve(h, d_, orders[d_][i], i % RD)),
                                  ([("scan", d_, i - RD)] if i >= RD else []) + ([("solve", d_, i - NFL)] if i >= NFL else [])))
                for d_ in range(2):
                    tasks.append((("scan", d_, i), (lambda d_=d_, i=i: scan(h, d_, orders[d_][i], i % RD)),
                                  [("solve", d_, i)] + ([("scan", d_, i - 1)] if i >= 1 else [])))
            for t in range(NT):
                tasks.append((("final", t), (lambda t=t: final(h, t, fin_bufs[t % 4])),
                              [("scan", 0, t), ("scan", 1, NT - 1 - t)] + ([("final", t - 4)] if t >= 4 else [])))
            if "noGDN" in dbg:
                tasks = []
            SCAN_REPS = ([int(x[2:]) for x in dbg if x.startswith("sr") and x[2:].isdigit()] or [1])[0]
            done, active, pending = set(), [], list(tasks)
            KACT = ([int(x[4:]) for x in dbg if x.startswith("kact")] or [12])[0]
            while pending or active:
                for tsk in list(pending):
                    if len(active) >= KACT:
                        break
                    if all(dd in done for dd in tsk[2]):
                        active.append((tsk[0], tsk[1]()))
                        pending.remove(tsk)
                active.sort(key=lambda it: 0 if it[0][0] == "scan" else 1)
                for item in list(active):
                    reps = SCAN_REPS if item[0][0] == "scan" else 1
                    for _ in range(reps):
                        try:
                            next(item[1])
                        except StopIteration:
                            active.remove(item)
                            done.add(item[0])
                            break
            if h == 0:
                dbg_dump("o_f", o_f[:], b_of, [128, NT, 128], BF16)
        P.barrier()
        if "o_bT" in dbg:
            d = dbg_tensor("o_bT", [4, 128, T], BF16)
            P.dma("sp", lambda e: e.dma_start(out=d, in_=ob_s), uname("dbg"))
            P.barrier()
        P.emit()
        SB.close()
        chk("stopB")

        S2 = Scope()
        wpa = S2.sb([64, 8, D], BF16, "wpa")
        wpb = S2.sb([128, 4, D], BF16, "wpb")
        wo = S2.sb([128, 8, D], BF16, "wo")
        b_wpa, b_wpb, b_wo = Buf(), Buf(), Buf()
        stg2 = S2.ring(2, [128, 2, D], F32, "stg2")
        si = [0]

        def load_plain(dst_fn, src_fn, n, rows, bdst):
            for j in range(n):
                st_, bst, tag = stg2[si[0] % 2]
                si[0] += 1
                P.dma("sp", lambda e, st_=st_, j=j: e.dma_start(out=st_[0:rows, :, :], in_=src_fn(j)), tag, writes=[bst])
                evac(dst_fn(j), st_[0:rows, :, :], [bst], [bdst])

        load_plain(lambda j: wpa[:, 2 * j:2 * j + 2, :], lambda j: w_proj_a[j * 128:(j + 1) * 128, :].rearrange("(h p) n -> p h n", p=64), 4, 64, b_wpa)
        load_plain(lambda j: wpb[:, 2 * j:2 * j + 2, :], lambda j: w_proj_b[j * 256:(j + 1) * 256, :].rearrange("(k p) n -> p k n", p=128), 2, 128, b_wpb)
        load_plain(lambda j: wo[:, 2 * j:2 * j + 2, :], lambda j: w_out[j * 256:(j + 1) * 256, :].rearrange("(k p) n -> p k n", p=128), 4, 128, b_wo)
        oaTr = S2.ring(2, [64, 8, 512], BF16, "oaT")
        obTr = S2.ring(2, [128, 4, 512], BF16, "obT")
        sgr = S2.ring(2, [128, 16, 512], BF16, "sgt")
        mTr = S2.ring(2, [128, 8, 512], BF16, "mT")
        mar = S2.ring(2, [128, 512], F32, "ma")
        mbr = S2.ring(2, [128, 512], F32, "mb")
        xr2 = S2.ring(4, [128, D], F32, "xr2")
        x1r = S2.ring(2, [128, D], F32, "x1r")
        ppar = S2.ring(2, [128, 512], F32, "ppa", psum=True)
        ppbr = S2.ring(2, [128, 512], F32, "ppb", psum=True)
        pxr = S2.ring(2, [128, 512], F32, "px", psum=True)

        def c2_loads(blk):
            bs = slice(blk * 512, (blk + 1) * 512)
            oaT, boaT, tag1 = oaTr[blk % 2]
            obT, bobT, tag2 = obTr[blk % 2]
            sgt, bsgt, tag3 = sgr[blk % 2]
            P.dma("sp", lambda e: e.dma_start(out=oaT[:], in_=oa_s[:, :, bs].rearrange("h p t -> p h t")), tag1, writes=[boaT])
            P.dma("sp", lambda e: e.dma_start(out=obT[:], in_=ob_s[:, :, bs].rearrange("h p t -> p h t")), tag2, writes=[bobT])
            P.dma("sp", lambda e: e.dma_start(out=sgt[:], in_=sg_s[:, :, bs].rearrange("g p t -> p g t")), tag3, writes=[bsgt])

        c2_loads(0)
        mc = [0]
        xc = [0]
        for blk in range(NB):
            oaT, boaT, tag1 = oaTr[blk % 2]
            obT, bobT, tag2 = obTr[blk % 2]
            sgt, bsgt, tag3 = sgr[blk % 2]
            mT, b_mT, _ = mTr[blk % 2]
            xts = []
            for i in range(4):
                tt = blk * 4 + i
                xt, bx, tagx = xr2[i]
                xts.append((xt, bx))
                P.dma("sp", lambda e, xt=xt, tt=tt: e.dma_start(out=xt[:], in_=x[tt * 128:(tt + 1) * 128, :]), tagx, writes=[bx])
            if blk + 1 < NB:
                c2_loads(blk + 1)
            for m in range(8):
                ms = slice(m * 128, (m + 1) * 128)
                ppa, b_ppa, _ = ppar[m % 2]
                ppb, b_ppb, _ = ppbr[m % 2]
                ma, b_ma, _ = mar[m % 2]
                mb, b_mb, _ = mbr[m % 2]
                for h in range(8):
                    mm(ppa[:], wpa[:, h, ms], oaT[:, h, :], h == 0, h == 7, [b_wpa, boaT], [b_ppa])
                for k in range(4):
                    mm(ppb[:], wpb[:, k, ms], obT[:, k, :], k == 0, k == 3, [b_wpb, bobT], [b_ppb])
                P.op("dve", lambda e, sgt=sgt, m=m, ma=ma, ppa=ppa: e.tensor_tensor(out=ma[:], in0=ppa[:], in1=sgt[:, m, :], op=ALU.mult),
                     reads=[b_ppa, bsgt], writes=[b_ma])
                P.op("dve", lambda e, sgt=sgt, m=m, mb=mb, ppb=ppb: e.tensor_tensor(out=mb[:], in0=ppb[:], in1=sgt[:, 8 + m, :], op=ALU.mult),
                     reads=[b_ppb, bsgt], writes=[b_mb])
                P.op("pool", lambda e, m=m, ma=ma, mb=mb, mT=mT: e.tensor_tensor(out=mT[:, m, :], in0=ma[:], in1=mb[:], op=ALU.add),
                     reads=[b_ma, b_mb], writes=[b_mT])
            for i in range(4):
                tt = blk * 4 + i
                xt, bx = xts[i]
                x1t, bx1, tagx1 = x1r[xc[0] % 2]
                xc[0] += 1
                for half in range(2):
                    px, bpx, _ = pxr[half]
                    for k in range(8):
                        mm(px[:], mT[:, k, i * 128:(i + 1) * 128], wo[:, k, half * 512:(half + 1) * 512], k == 0, k == 7, [b_mT, b_wo], [bpx])
                    P.op("dve", lambda e, x1t=x1t, xt=xt, px=px, half=half: e.tensor_tensor(out=x1t[:, half * 512:(half + 1) * 512], in0=px[:],
                                                                                           in1=xt[:, half * 512:(half + 1) * 512], op=ALU.add),
                         reads=[bpx, bx], writes=[bx1])
                P.dma("sp", lambda e, x1t=x1t, tt=tt: e.dma_start(out=x1_s[tt * 128:(tt + 1) * 128, :], in_=x1t[:]), tagx1, reads=[bx1])
        P.barrier()
        if "x1" in dbg:
            d = dbg_tensor("x1", [T, D], F32)
            P.dma("sp", lambda e: e.dma_start(out=d, in_=x1_s), uname("dbg"))
            P.barrier()
        P.emit()
        S2.close()
        chk("stopC2")

        SD = Scope()
        w1 = SD.sb([128, 8, DFF], BF16, "w1")
        w2 = SD.sb([128, 32, D], BF16, "w2")
        b_w1, b_w2 = Buf(), Buf()
        w1v = w_ff1.rearrange("(k p) n -> p k n", p=128)
        w2v = w_ff2.rearrange("(k p) n -> p k n", p=128)
        b_w1g = [Buf() for _ in range(8)]
        b_w2g = [Buf() for _ in range(8)]
        for gq in range(8):
            P.dma("pool", lambda e, gq=gq: e.dma_start(out=w1[:, :, gq * 512:(gq + 1) * 512], in_=w1v[:, :, gq * 512:(gq + 1) * 512]), "w1g%d" % gq, writes=[b_w1g[gq]])
        for gq in range(8):
            P.dma("pool", lambda e, gq=gq: e.dma_start(out=w2[:, 4 * gq:4 * gq + 4, :], in_=w2v[:, 4 * gq:4 * gq + 4, :]), "w2g%d" % gq, writes=[b_w2g[gq]])
        n2w = SD.sb([128, D], F32, "n2w")
        b_n2w = Buf()
        P.dma("sp", lambda e: e.dma_start(out=n2w[:], in_=norm2_w.rearrange("(o n) -> o n", o=1).to_broadcast([128, D])), P.new_tag(), writes=[b_n2w])
        fnw = SD.sb([128, D], F32, "fnw")
        b_fnw = Buf()
        P.dma("sp", lambda e: e.dma_start(out=fnw[:], in_=final_norm_w[0:1, :].to_broadcast([128, D])), P.new_tag(), writes=[b_fnw])
        TB = 256
        NBD = T // TB
        x1d = SD.ring(4, [128, D], F32, "x1d")
        h2r = SD.ring(2, [128, D], BF16, "h2")
        junkd = SD.sb([128, D], BF16, "junkd")
        b_junkd = Buf()
        sd = SD.sb([128, NT], F32, "sd")
        b_sd = [Buf() for _ in range(NT)]
        sd2 = SD.sb([128, NT], F32, "sd2")
        b_sd2 = [Buf() for _ in range(NT)]
        h2Tr = SD.ring(2, [128, 8, TB], BF16, "h2T")
        aT = SD.sb([128, 32, TB], BF16, "aT")
        b_aT = Buf()
        rr = SD.ring(2, [128, TB], F32, "rr")
        yr = SD.ring(2, [128, D], F32, "yr")
        pTd = SD.ring(2, [128, 8, 128], BF16, "pTd", psum=True)
        pfr = SD.ring(2, [128, TB], F32, "pf", psum=True)
        pyr = SD.ring(2, [128, 512], F32, "py", psum=True)
        xdc = [0]
        NTB = TB // 128
        tiles_of = {}

        def d_front(bd):
            h2T, b_h2T, _ = h2Tr[bd % 2]
            tiles = []
            for i in range(NTB):
                tt = bd * NTB + i
                x1t, bx1, tagx1 = x1d[xdc[0] % 4]
                h2, bh2, _ = h2r[xdc[0] % 2]
                pTt, bpT, _ = pTd[xdc[0] % 2]
                xdc[0] += 1
                tiles.append((tt, x1t, bx1))
                P.dma("sp", lambda e, x1t=x1t, tt=tt: e.dma_start(out=x1t[:], in_=x1_s[tt * 128:(tt + 1) * 128, :]), tagx1, writes=[bx1])
                P.op("act", lambda e, x1t=x1t, tt=tt: e.activation(out=junkd[:], in_=x1t[:], func=AF.Square, accum_out=sd[:, tt:tt + 1]),
                     reads=[bx1], writes=[b_junkd, b_sd[tt]])
                P.op("act", lambda e, tt=tt: e.activation(out=sd[:, tt:tt + 1], in_=sd[:, tt:tt + 1], func=AF.Sqrt, scale=1.0 / D, bias=EPS),
                     reads=[b_sd[tt]], writes=[b_sd[tt]])
                P.op("dve", lambda e, tt=tt: e.reciprocal(out=sd[:, tt:tt + 1], in_=sd[:, tt:tt + 1]), reads=[b_sd[tt]], writes=[b_sd[tt]])
                P.op("dve", lambda e, x1t=x1t, h2=h2, tt=tt: e.scalar_tensor_tensor(out=h2[:], in0=x1t[:], scalar=sd[:, tt:tt + 1], in1=n2w[:], op0=ALU.mult, op1=ALU.mult),
                     reads=[bx1, b_sd[tt], b_n2w], writes=[bh2])
                for k in range(8):
                    P.op("pe", lambda e, k=k, h2=h2, pTt=pTt: e.transpose(out=pTt[:, k, :], in_=h2[:, k * 128:(k + 1) * 128], identity=identb),
                         reads=[bh2, b_cb], writes=[bpT])
                P.op("dve", lambda e, pTt=pTt, i=i, h2T=h2T: e.tensor_copy(out=h2T[:, :, i * 128:(i + 1) * 128], in_=pTt[:]), reads=[bpT], writes=[b_h2T])
            tiles_of[bd] = tiles

        d_front(0)
        for bd in range(NBD):
            h2T, b_h2T, _ = h2Tr[bd % 2]
            for f in range(32):
                pf, bpf, _ = pfr[f % 2]
                r_, br_, _ = rr[f % 2]
                for k in range(8):
                    mm(pf[:], w1[:, k, f * 128:(f + 1) * 128], h2T[:, k, :], k == 0, k == 7, [b_w1g[f // 4], b_h2T], [bpf])
                P.op("act", lambda e, pf=pf, r_=r_: e.activation(out=r_[:], in_=pf[:], func=AF.Relu), reads=[bpf], writes=[br_])
                P.op("pool", lambda e, r_=r_, f=f: e.tensor_tensor(out=aT[:, f, :], in0=r_[:], in1=r_[:], op=ALU.mult), reads=[br_], writes=[b_aT])
            if bd + 1 < NBD:
                d_front(bd + 1)
            for i, (tt, x1t, bx1) in enumerate(tiles_of[bd]):
                yt, byt, tagy = yr[tt % 2]
                for half in range(2):
                    py, bpy, _ = pyr[half]
                    for f in range(32):
                        mm(py[:], aT[:, f, i * 128:(i + 1) * 128], w2[:, f, half * 512:(half + 1) * 512], f == 0, f == 31, [b_aT, b_w2g[f // 4]], [bpy])
                    P.op("dve", lambda e, yt=yt, x1t=x1t, py=py, half=half: e.tensor_tensor(out=yt[:, half * 512:(half + 1) * 512], in0=py[:],
                                                                                           in1=x1t[:, half * 512:(half + 1) * 512], op=ALU.add),
                         reads=[bpy, bx1], writes=[byt])
                P.op("act", lambda e, yt=yt, tt=tt: e.activation(out=junkd[:], in_=yt[:], func=AF.Square, accum_out=sd2[:, tt:tt + 1]),
                     reads=[byt], writes=[b_junkd, b_sd2[tt]])
                P.op("act", lambda e, tt=tt: e.activation(out=sd2[:, tt:tt + 1], in_=sd2[:, tt:tt + 1], func=AF.Sqrt, scale=1.0 / D, bias=EPS),
                     reads=[b_sd2[tt]], writes=[b_sd2[tt]])
                P.op("dve", lambda e, tt=tt: e.reciprocal(out=sd2[:, tt:tt + 1], in_=sd2[:, tt:tt + 1]), reads=[b_sd2[tt]], writes=[b_sd2[tt]])
                P.op("dve", lambda e, yt=yt, tt=tt: e.scalar_tensor_tensor(out=yt[:], in0=yt[:], scalar=sd2[:, tt:tt + 1], in1=fnw[:], op0=ALU.mult, op1=ALU.mult),
                     reads=[byt, b_sd2[tt], b_fnw], writes=[byt])
                P.dma("sp", lambda e, yt=yt, tt=tt: e.dma_start(out=y[tt * 128:(tt + 1) * 128, :], in_=yt[:]), tagy, reads=[byt])
        P.barrier()
        P.emit(final=True)
        SD.close()
        return nc, dbg_out
    except Stop:
        P.barrier()
        P.emit(final=True)
        return nc, dbg_out


_CST = None


def core_inputs(inp, b, T):
    global _CST
    if _CST is None:
        _CST = make_consts()
    f = lambda a: np.ascontiguousarray(a, dtype=np.float32)
    return {
        "x": f(inp["x"][b, :T]),
        "positions": np.ascontiguousarray(inp["positions"][b, :T], dtype=np.int32).reshape(1, T),
        "norm1_w": f(inp["norm1_w"][0]), "w_in": f(inp["w_in"][0]), "q_norm_w": f(inp["q_norm_w"][0]),
        "w_uq": f(inp["w_uq"][0]), "kv_norm_w": f(inp["kv_norm_w"][0]), "w_ukv": f(inp["w_ukv"][0]),
        "conv_w": f(inp["conv_w"][0]),
        "a_log_f": f(inp["a_log_f"]), "dt_bias_f": f(inp["dt_bias_f"]),
        "a_log_b": f(inp["a_log_b"]), "dt_bias_b": f(inp["dt_bias_b"]),
        "gdn_norm_w": f(inp["gdn_norm_w"]),
        "w_proj_a": f(inp["w_proj_a"][0]), "w_proj_b": f(inp["w_proj_b"][0]), "w_out": f(inp["w_out"][0]),
        "norm2_w": f(inp["norm2_w"][0]), "w_ff1": f(inp["w_ff1"][0]), "w_ff2": f(inp["w_ff2"][0]),
        "final_norm_w": f(inp["final_norm_w"]).reshape(1, D),
        "cst": _CST,
    }


def kernel(**inputs):
    T = 4096
    nc, _ = build(T=T)
    maps = [core_inputs(inputs, b, T) for b in range(8)]
    res = run_bass_kernel_spmd(nc, maps, core_ids=list(range(8)))
    return np.stack([np.asarray(r["y"], dtype=np.float32) for r in res.results], axis=0)
```

```python
import math
from contextlib import ExitStack

import numpy as np
import concourse.bass as bass
import concourse.mybir as mybir
from concourse.bass_utils import run_bass_kernel_spmd

F32 = mybir.dt.float32
BF16 = mybir.dt.bfloat16
I32 = mybir.dt.int32
AF = mybir.ActivationFunctionType
ALU = mybir.AluOpType
AX = mybir.AxisListType

D = 1024
D_IN = 4528
DFF = 4096
EPS = 1e-6
C_Q, C_KV, C_KR, C_QKV, C_G, C_GG, C_GA, C_GB = 0, 256, 384, 416, 1952, 1968, 2480, 3504

COMPUTE = ("pe", "act", "dve", "pool")
QUEUES = ("sp",)


class Buf:
    __slots__ = ("w", "r", "excl")

    def __init__(self, excl=False):
        self.w = {}
        self.r = {}
        self.excl = excl


class Prog:
    def __init__(self, nc):
        self.nc = nc
        self.ins = {e: [] for e in COMPUTE + QUEUES}
        self.tags = {}
        self.tag_list = []
        self.emitted = {e: 0 for e in COMPUTE + QUEUES}
        self.waited = {e: {} for e in COMPUTE + QUEUES}
        self.cnt = {e: 0 for e in COMPUTE}
        self.vals = {e: [] for e in COMPUTE}
        self.esem = None
        self.tsem = {}
        self.semstack = None
        self.tagctr = 0

    def new_tag(self):
        self.tagctr += 1
        return "q%d" % self.tagctr

    def _deps(self, reads, writes):
        deps = []
        for b in reads:
            deps.extend(ev + (True,) for ev in b.w.values())
        for b in writes:
            deps.extend(ev + (False,) for ev in b.w.values())
            deps.extend(ev + (False,) for ev in b.r.values())
        return deps

    def op(self, eng, fn, reads=(), writes=()):
        xr = [b for b in reads if b.excl]
        if xr:
            writes = list(writes) + xr
            reads = [b for b in reads if not b.excl]
        deps = self._deps(reads, writes)
        for b in xr:
            deps.extend(ev + (True,) for ev in b.w.values())
        idx = len(self.ins[eng])
        ev = ("e", eng, idx)
        self.ins[eng].append({"fn": fn, "deps": deps, "kind": "c", "need": False})
        for b in reads:
            b.r[eng] = ev
        for b in writes:
            b.w = {eng: ev}
            b.r = {}
        return ev

    def dma(self, q, fn, tag, reads=(), writes=()):
        if q != "sp":
            tag = "sw_" + str(tag)
        deps = self._deps(reads, writes)
        if tag not in self.tags:
            self.tags[tag] = 0
            self.tag_list.append(tag)
        self.tags[tag] += 1
        ev = ("d", tag, self.tags[tag])
        self.ins[q].append({"fn": fn, "deps": deps, "kind": "d", "tag": tag, "need": False})
        key = ("d", tag)
        for b in reads:
            b.r[key] = ev
        for b in writes:
            b.w = {key: ev}
            b.r = {}
        return ev

    def barrier(self):
        last = []
        for e in COMPUTE:
            if self.ins[e]:
                for i in range(len(self.ins[e]) - 1, -1, -1):
                    if self.ins[e][i]["kind"] == "c":
                        last.append(("e", e, i, True))
                        break
        for t in self.tag_list:
            last.append(("d", t, self.tags[t], True))
        for e in COMPUTE + QUEUES:
            self.ins[e].append({"fn": None, "deps": list(last), "kind": "b", "need": False})
        self.tagctr = 0

    def emit(self, final=False):
        nc = self.nc
        if self.semstack is None:
            self.semstack = ExitStack()
            self.esem = {e: self.semstack.enter_context(nc.semaphore("s_" + e)) for e in COMPUTE}
        for t in self.tag_list:
            if t not in self.tsem:
                self.tsem[t] = self.semstack.enter_context(nc.semaphore("t_" + str(t)))
        for e, lst in self.ins.items():
            for rec in lst[self.emitted[e]:]:
                for d in rec["deps"]:
                    if d[0] == "e" and not (d[1] == e and (e == "pe" or not d[3])):
                        self.ins[d[1]][d[2]]["need"] = True
        for e in COMPUTE:
            lst = self.ins[e]
            lastc = None
            for i in range(self.emitted[e], len(lst)):
                if lst[i]["kind"] == "c":
                    lastc = i
            for i in range(self.emitted[e], len(lst)):
                rec = lst[i]
                if rec["kind"] == "c" and (rec["need"] or i == lastc):
                    rec["need"] = True
                    self.cnt[e] += 1
                self.vals[e].append(self.cnt[e])

        def run(e, engobj):
            waited = self.waited[e]
            lst = self.ins[e]
            for i in range(self.emitted[e], len(lst)):
                rec = lst[i]
                for d in rec["deps"]:
                    if d[0] == "e":
                        if d[1] == e and (e == "pe" or not d[3]):
                            continue
                        val = self.vals[d[1]][d[2]]
                        sem, key = self.esem[d[1]], d[1]
                    else:
                        sem, val, key = self.tsem[d[1]], 16 * d[2], ("d", d[1])
                    if waited.get(key, 0) >= val:
                        continue
                    waited[key] = val
                    engobj.wait_ge(sem, val)
                if rec["kind"] == "b":
                    continue
                inst = rec["fn"](engobj)
                if rec["kind"] == "d":
                    inst.then_inc(self.tsem[rec["tag"]], 16)
                elif rec["need"]:
                    inst.then_inc(self.esem[e], 1)
            if final and e == "sp":
                for t in self.tag_list:
                    val = 16 * self.tags[t]
                    if waited.get(("d", t), 0) < val:
                        engobj.wait_ge(self.tsem[t], val)
            self.emitted[e] = len(lst)

        with nc.Block() as block:
            block.tensor(lambda eng: run("pe", eng))
            block.scalar(lambda eng: run("act", eng))
            block.vector(lambda eng: run("dve", eng))
            block.gpsimd(lambda eng: run("pool", eng))
            block.sync(lambda eng: run("sp", eng))
        if final:
            self.semstack.close()


CONST_NAMES = ["ident", "ones", "MLi", "MUi", "NOTEYE", "BD", "BIGf", "BIGb", "misc"]
BIG = 30000.0


def make_consts():
    p = np.arange(128)[:, None]
    f = np.arange(128)[None, :]
    same = (p // 64) == (f // 64)
    c = {}
    c["ident"] = (p == f)
    c["ones"] = np.ones((128, 128))
    c["MLi"] = (p >= f) & same
    c["MUi"] = (p <= f) & same
    c["NOTEYE"] = (p != f)
    c["BD"] = same
    c["BIGf"] = BIG * (~((p >= f) & same))
    c["BIGb"] = BIG * (~((p <= f) & same))
    misc = np.zeros((128, 128))
    misc[:, 0] = (np.arange(128) < 64)
    misc[:, 1] = (np.arange(128) >= 64)
    inv_freq = 1.0 / (10000.0 ** (np.arange(0, 32, 2, dtype=np.float32) / 32.0))
    misc[:, 2] = np.tile(inv_freq.astype(np.float32), 8)
    misc[64, 64:128] = 1.0
    c["misc"] = misc
    return np.concatenate([np.asarray(c[n], dtype=np.float32) for n in CONST_NAMES], axis=1)


def build(T=4096, dbg=()):
    NT = T // 128
    NB = T // 512
    nc = bass.Bass("TRN2", target_bir_lowering=False)

    def din(name, shape, dt=F32):
        return nc.dram_tensor(name, list(shape), dt, kind="ExternalInput").ap()

    x = din("x", [T, D])
    pos = din("positions", [1, T], I32)
    norm1_w = din("norm1_w", [D])
    w_in = din("w_in", [D, D_IN])
    q_norm_w = din("q_norm_w", [256])
    w_uq = din("w_uq", [256, 768])
    kv_norm_w = din("kv_norm_w", [128])
    w_ukv = din("w_ukv", [128, 1024])
    conv_w = din("conv_w", [5, 1536])
    a_log_f = din("a_log_f", [1, 4])
    dt_bias_f = din("dt_bias_f", [1, 4])
    a_log_b = din("a_log_b", [1, 4])
    dt_bias_b = din("dt_bias_b", [1, 4])
    gdn_norm_w = din("gdn_norm_w", [1, 128])
    w_proj_a = din("w_proj_a", [512, D])
    w_proj_b = din("w_proj_b", [512, D])
    w_out = din("w_out", [D, D])
    norm2_w = din("norm2_w", [D])
    w_ff1 = din("w_ff1", [D, DFF])
    w_ff2 = din("w_ff2", [DFF, D])
    final_norm_w = din("final_norm_w", [1, D])
    cst = din("cst", [128, 128 * len(CONST_NAMES)])
    y = nc.dram_tensor("y", [T, D], F32, kind="ExternalOutput").ap()

    def dscr(name, shape, dt):
        return nc.dram_tensor(name, list(shape), dt, kind="Internal").ap()

    zq_s = dscr("zq_s", [12, 128, T + 4], BF16)
    sgdn_s = dscr("sgdn_s", [T, 512], BF16)
    sg_s = dscr("sg_s", [16, 128, T], BF16)
    oa_s = dscr("oa_s", [8, 64, T], BF16)
    x1_s = dscr("x1_s", [T, D], F32)
    ob_s = dscr("ob_s", [4, 128, T], BF16)
    gcd_s = dscr("gcd_s", [2, T // 128, 128], F32)
    wff1_s = dscr("wff1_s", [128, 8, DFF], BF16)
    wff2_s = dscr("wff2_s", [128, 32, D], BF16)

    dbg_out = {}

    def dbg_tensor(name, shape, dt=F32):
        dbg_out[name] = nc.dram_tensor("dbg_" + name, list(shape), dt, kind="ExternalOutput").ap()
        return dbg_out[name]

    P = Prog(nc)
    ctr = [0]

    def uname(s):
        ctr[0] += 1
        return "%s_%d" % (s, ctr[0])

    class Scope:
        def __init__(self):
            self.st = ExitStack()

        def sb(self, shape, dt, name="t"):
            return self.st.enter_context(nc.sbuf_tensor(uname(name), list(shape), dt))

        def ps(self, shape, dt=F32, name="p"):
            return self.st.enter_context(nc.psum_tensor(uname(name), list(shape), dt))

        def ring(self, n, shape, dt, name="r", psum=False):
            out = []
            for i in range(n):
                t = self.ps(shape, dt, name) if psum else self.sb(shape, dt, name)
                out.append((t, Buf(), P.new_tag()))
            return out

        def close(self):
            self.st.close()

    class Stop(Exception):
        pass

    def chk(name):
        if name in dbg:
            raise Stop()

    G = Scope()
    cst_t = G.sb([128, 128 * len(CONST_NAMES)], F32, "cst")
    b_cst = Buf()
    P.dma("sp", lambda e: e.dma_start(out=cst_t[:], in_=cst[:, :]), "cst", writes=[b_cst])

    def CF(name, rows=slice(0, 128), cols=slice(0, 128)):
        i = CONST_NAMES.index(name)
        c0 = i * 128 + cols.start
        c1 = i * 128 + cols.stop
        return cst_t[rows, c0:c1]

    cb_t = G.sb([128, 256], BF16, "cstb")
    b_cb = Buf()
    P.op("dve", lambda e: e.tensor_copy(out=cb_t[:], in_=cst_t[:, 0:256]), reads=[b_cst], writes=[b_cb])
    identb = cb_t[:, 0:128]
    onesb = cb_t[:, 128:256]
    bigb_t = G.sb([128, 256], BF16, "bigb")
    P.op("dve", lambda e: e.tensor_copy(out=bigb_t[:], in_=cst_t[:, 6 * 128:8 * 128]), reads=[b_cst], writes=[b_cb])

    gall = G.sb([128, NT, 16], F32, "gall")
    GA = Scope()
    cqnT = GA.sb([128, 2, T], BF16, "cqnT")
    ckvnT = GA.sb([128, T], BF16, "ckvnT")
    kropeT_full = GA.sb([96, T], BF16, "kropeT")
    kropeT = kropeT_full[64:96, :]
    cosT_full = GA.sb([96, T], BF16, "cosT")
    cosT = cosT_full[64:96, :]
    sinT_full = GA.sb([96, T], BF16, "sinT")
    sinT = sinT_full[64:96, :]
    b_cqn = [Buf() for _ in range(NB)]
    b_ckvn = [Buf() for _ in range(NB)]
    b_krope = [Buf() for _ in range(NB)]
    b_rope = Buf()
    b_gall = [Buf() for _ in range(NB)]

    alt = [0]

    def evac(out, in_, reads, writes, eng=None):
        if eng is None:
            alt[0] ^= 1
            eng = "act" if alt[0] else "dve"
        if eng == "act":
            P.op("act", lambda e: e.copy(out=out, in_=in_), reads=reads, writes=writes)
        elif eng == "dve":
            P.op("dve", lambda e: e.tensor_copy(out=out, in_=in_), reads=reads, writes=writes)
        else:
            P.op("pool", lambda e: e.tensor_copy(out=out, in_=in_), reads=reads, writes=writes)

    def scale_cast(out, in_, scale_ap, reads, writes, eng=None):
        if eng is None:
            alt[0] ^= 1
            eng = "act" if alt[0] else "dve"
        if scale_ap is None:
            return evac(out, in_, reads, writes, eng)
        if eng == "act":
            P.op("act", lambda e: e.activation(out=out, in_=in_, func=AF.Copy, scale=scale_ap), reads=reads, writes=writes)
        else:
            P.op("dve", lambda e: e.tensor_scalar(out=out, in0=in_, scalar1=scale_ap, scalar2=None, op0=ALU.mult),
                 reads=reads, writes=writes)

    def mm(out, lhsT, rhs, start, stop, reads, writes):
        P.op("pe", lambda e: e.matmul(out, lhsT=lhsT, rhs=rhs, start=start, stop=stop), reads=reads, writes=writes)

    def dbg_dump(name, src_ap, reads, shape, dt=F32):
        if name in dbg:
            d = dbg_tensor(name, shape, dt)
            P.dma("sp", lambda e: e.dma_start(out=d, in_=src_ap), uname("dbg"), reads=reads)

    def load_cols(scope, src, n, name):
        k = n // 128
        t = scope.sb([128, k], F32, name)
        b = Buf()
        P.dma("sp", lambda e: e.dma_start(out=t[:], in_=src.rearrange("(k p) -> p k", p=128), allow_slow_non_contiguous=True),
              P.new_tag(), writes=[b])
        return t, b

    TWO_PI = 2.0 * math.pi
    RC = min(T, 1024)

    def rope_gen(scope):
        pos_i = scope.sb([96, RC], I32, "pos_i")[64:96, :]
        u0 = scope.sb([96, RC], F32, "u0")[64:96, :]
        u1 = scope.sb([96, RC], F32, "u1")[64:96, :]
        ki = scope.sb([96, RC], I32, "ki")[64:96, :]
        b_pos, b_u0, b_u1, b_ki = Buf(), Buf(), Buf(), Buf()
        invf = CF("misc", slice(64, 96), slice(2, 3))
        tagp = P.new_tag()
        pf = pos_i.bitcast(F32)
        for c0 in range(0, T, RC):
            cs_ = slice(c0, c0 + RC)
            P.dma("sp", lambda e, cs_=cs_: e.dma_start(out=pos_i, in_=pos[0:1, cs_].to_broadcast([32, RC])), tagp, writes=[b_pos])
            P.op("dve", lambda e: e.tensor_copy(out=u0, in_=pos_i), reads=[b_pos], writes=[b_u0])
            P.op("dve", lambda e: e.tensor_scalar(out=u0, in0=u0, scalar1=invf, scalar2=1.0 / TWO_PI, op0=ALU.mult, op1=ALU.mult),
                 reads=[b_u0, b_cst], writes=[b_u0])
            yield
            for (dst, shift) in ((sinT, 0.0), (cosT, 0.25)):
                P.op("dve", lambda e, shift=shift: e.tensor_scalar(out=u1, in0=u0, scalar1=shift, scalar2=None, op0=ALU.add), reads=[b_u0], writes=[b_u1])
                P.op("dve", lambda e: e.tensor_copy(out=ki, in_=u1), reads=[b_u1], writes=[b_ki])
                yield
                P.op("dve", lambda e: e.tensor_copy(out=pf, in_=ki), reads=[b_ki], writes=[b_pos])
                P.op("dve", lambda e: e.tensor_tensor(out=u1, in0=u1, in1=pf, op=ALU.subtract), reads=[b_u1, b_pos], writes=[b_u1])
                yield
                P.op("dve", lambda e: e.tensor_scalar(out=pf, in0=u1, scalar1=0.5, scalar2=None, op0=ALU.is_gt), reads=[b_u1], writes=[b_pos])
                P.op("dve", lambda e: e.tensor_tensor(out=u1, in0=u1, in1=pf, op=ALU.subtract), reads=[b_u1, b_pos], writes=[b_u1])
                yield
                P.op("dve", lambda e: e.tensor_scalar(out=pf, in0=u1, scalar1=-0.5, scalar2=None, op0=ALU.is_lt), reads=[b_u1], writes=[b_pos])
                P.op("dve", lambda e: e.tensor_tensor(out=u1, in0=u1, in1=pf, op=ALU.add), reads=[b_u1, b_pos], writes=[b_u1])
                P.op("act", lambda e, dst=dst, cs_=cs_: e.activation(out=dst[:, cs_], in_=u1, func=AF.Sin, scale=TWO_PI * (1.0 - 2e-6)),
                     reads=[b_u1], writes=[b_rope])
                yield

    try:
        SA = Scope()
        xnT = SA.sb([128, 8, T], BF16, "xnT")
        b_xnT = [Buf() for _ in range(NB)]
        n1w = SA.sb([128, D], F32, "n1w")
        b_n1w = Buf()
        P.dma("sp", lambda e: e.dma_start(out=n1w[:], in_=norm1_w.rearrange("(o n) -> o n", o=1).to_broadcast([128, D])), P.new_tag(), writes=[b_n1w])
        xring = SA.ring(2, [128, D], F32, "xr")
        xnring = SA.ring(2, [128, D], BF16, "xn")
        junk = SA.sb([128, D], BF16, "junk")
        b_junk = Buf()
        ss = SA.sb([128, NT], F32, "ss")
        rs = SA.sb([128, NT], F32, "rs")
        b_ss = [Buf() for _ in range(NT)]
        pT = SA.ring(2, [128, 8, 128], BF16, "pT", psum=True)
        pz = SA.ring(2, [128, 512], F32, "pz", psum=True)
        pss_t = SA.ps([128, 512], F32, "pss")
        b_pss = Buf()
        pk_a = SA.ps([96, 512], F32, "pka")[64:96, :]
        pk_b = SA.ps([96, 512], F32, "pkb")[64:96, :]
        b_pk = Buf()
        wbr = SA.ring(3, [128, 8, 256], BF16, "wb")

        rg = [rope_gen(SA)]
        for t in range(NT):
            xt, bx, tagx = xring[t % 2]
            xn, bxn, _ = xnring[t % 2]
            pTt, bpT, _ = pT[t % 2]
            P.dma("sp", lambda e, xt=xt, t=t: e.dma_start(out=xt[:], in_=x[t * 128:(t + 1) * 128, :]), tagx, writes=[bx])
            P.op("act", lambda e, xt=xt, t=t: e.activation(out=junk[:], in_=xt[:], func=AF.Square, accum_out=ss[:, t:t + 1]),
                 reads=[bx], writes=[b_junk, b_ss[t]])
            P.op("act", lambda e, t=t: e.activation(out=rs[:, t:t + 1], in_=ss[:, t:t + 1], func=AF.Sqrt, scale=1.0 / D, bias=EPS),
                 reads=[b_ss[t]], writes=[b_ss[t]])
            P.op("dve", lambda e, t=t: e.reciprocal(out=rs[:, t:t + 1], in_=rs[:, t:t + 1]), reads=[b_ss[t]], writes=[b_ss[t]])
            P.op("dve", lambda e, xt=xt, xn=xn, t=t: e.scalar_tensor_tensor(out=xn[:], in0=xt[:], scalar=rs[:, t:t + 1], in1=n1w[:], op0=ALU.mult, op1=ALU.mult),
                 reads=[bx, b_ss[t], b_n1w], writes=[bxn])
            for k in range(8):
                P.op("pe", lambda e, k=k, xn=xn, pTt=pTt: e.transpose(out=pTt[:, k, :], in_=xn[:, k * 128:(k + 1) * 128], identity=identb),
                     reads=[bxn, b_cb], writes=[bpT])
            evac(xnT[:, :, t * 128:(t + 1) * 128], pTt[:], [bpT], [b_xnT[t // 4]])
            for _ in range(2):
                if rg[0] is not None:
                    try:
                        next(rg[0])
                    except StopIteration:
                        rg[0] = None
        if rg[0] is not None:
            for _ in rg[0]:
                pass
        dbg_dump("cosT", cosT, [b_rope], [32, T], BF16)
        dbg_dump("sinT", sinT, [b_rope], [32, T], BF16)

        chk("stopA1")
        gctr = [0]
        w_in_v = w_in.rearrange("(k p) n -> p k n", p=128)

        def load_group(c0, ncol):
            i = gctr[0]
            gctr[0] += 1
            wb, bwb, tag = wbr[i % 3]
            P.dma("pool", lambda e: e.dma_start(out=wb[:, :, 0:ncol], in_=w_in_v[:, :, c0:c0 + ncol]), tag, writes=[bwb])
            return wb, bwb

        def proj_fm(wb, bwb, c_lo, ncol, blk, ptile, bp):
            for k in range(8):
                mm(ptile, wb[:, k, c_lo:c_lo + ncol], xnT[:, k, blk * 512:(blk + 1) * 512], k == 0, k == 7,
                   [bwb, b_xnT[blk]], [bp])

        cqf = SA.sb([128, 2, 512], F32, "cqf")
        b_cqf_m = [Buf(), Buf()]
        sqb = SA.sb([128, 2, 512], BF16, "sqb")
        b_sqb_m = [Buf(), Buf()]
        rq = SA.sb([128, 512], F32, "rq")
        b_rq = Buf()
        zc = [0]

        def next_pz():
            zc[0] += 1
            return pz[zc[0] % 2]

        def norm_proj(c0, nchunk, dst_fn, dst_bufs):
            wb, bwb = load_group(c0, nchunk * 128)
            chk("stopA2a")
            for blk in range(NB):
                for m in range(nchunk):
                    pzt, bpz, _ = next_pz()
                    proj_fm(wb, bwb, m * 128, 128, blk, pzt[:], bpz)
                    chk("stopA2m")
                    P.op("act", lambda e, pzt=pzt, m=m: e.copy(out=cqf[:, m, :], in_=pzt[:]), reads=[bpz], writes=[b_cqf_m[m]])
                    P.op("pool", lambda e, m=m: e.tensor_tensor(out=sqb[:, m, :], in0=cqf[:, m, :], in1=cqf[:, m, :], op=ALU.mult),
                         reads=[b_cqf_m[m]], writes=[b_sqb_m[m]])
                    chk("stopA2n")
                chk("stopA2b_")
                for m in range(nchunk):
                    mm(pss_t[:], onesb, sqb[:, m, :], m == 0, m == nchunk - 1, [b_sqb_m[m], b_cb], [b_pss])
                chk("stopA2c")
                P.op("act", lambda e: e.activation(out=rq[:], in_=pss_t[:], func=AF.Sqrt, scale=1.0 / (nchunk * 128), bias=EPS),
                     reads=[b_pss], writes=[b_rq])
                chk("stopA2d")
                P.op("dve", lambda e: e.reciprocal(out=rq[:], in_=rq[:]), reads=[b_rq], writes=[b_rq])
                chk("stopA2e")
                for m in range(nchunk):
                    P.op("dve", lambda e, m=m, blk=blk: e.tensor_tensor(out=dst_fn(m, blk), in0=cqf[:, m, :], in1=rq[:], op=ALU.mult),
                         reads=[b_cqf_m[m], b_rq], writes=[dst_bufs[blk]])

        norm_proj(C_Q, 2, lambda m, blk: cqnT[:, m, blk * 512:(blk + 1) * 512], b_cqn)
        chk("stopA2f")
        norm_proj(C_KV, 1, lambda m, blk: ckvnT[:, blk * 512:(blk + 1) * 512], b_ckvn)
        dbg_dump("cqnT", cqnT[:], b_cqn, [128, 2, T], BF16)
        dbg_dump("ckvnT", ckvnT[:], b_ckvn, [128, T], BF16)

        chk("stopA2")
        wb, bwb = load_group(C_KR, 32)
        P.op("dve", lambda e, wb=wb: e.tensor_scalar(out=wb[:, :, 32:48], in0=wb[:, :, 16:32], scalar1=-1.0, scalar2=None, op0=ALU.mult),
             reads=[bwb], writes=[bwb])
        P.op("dve", lambda e, wb=wb: e.tensor_copy(out=wb[:, :, 48:64], in_=wb[:, :, 0:16]), reads=[bwb], writes=[bwb])
        t1 = SA.sb([96, 512], F32, "t1")[64:96, :]
        t2 = SA.sb([96, 512], F32, "t2")[64:96, :]
        b_t1, b_t2 = Buf(), Buf()
        for blk in range(NB):
            bs = slice(blk * 512, (blk + 1) * 512)
            proj_fm(wb, bwb, 0, 32, blk, pk_a, b_pk)
            proj_fm(wb, bwb, 32, 32, blk, pk_b, b_pk)
            P.op("dve", lambda e, bs=bs: e.tensor_tensor(out=t1, in0=pk_a, in1=cosT[:, bs], op=ALU.mult),
                 reads=[b_pk, b_rope], writes=[b_t1])
            P.op("dve", lambda e, bs=bs: e.tensor_tensor(out=t2, in0=pk_b, in1=sinT[:, bs], op=ALU.mult),
                 reads=[b_pk, b_rope], writes=[b_t2])
            P.op("dve", lambda e, bs=bs: e.tensor_tensor(out=kropeT[:, bs], in0=t1, in1=t2, op=ALU.add),
                 reads=[b_t1, b_t2], writes=[b_krope[blk]])
        dbg_dump("kropeT", kropeT, b_krope, [32, T], BF16)

        chk("stopA2b")
        dtb = SA.sb([128, 8], F32, "dtb")
        nega = SA.sb([128, 8], F32, "nega")
        b_dtb, b_nega = Buf(), Buf()
        for j, (src_dt, src_al) in enumerate(((dt_bias_f, a_log_f), (dt_bias_b, a_log_b))):
            P.dma("sp", lambda e, j=j, src_dt=src_dt: e.dma_start(out=dtb[:, j * 4:(j + 1) * 4], in_=src_dt[0:1, :].to_broadcast([128, 4])),
                  "gsm_dt", writes=[b_dtb])
            P.dma("sp", lambda e, j=j, src_al=src_al: e.dma_start(out=nega[:, j * 4:(j + 1) * 4], in_=src_al[0:1, :].to_broadcast([128, 4])),
                  "gsm_al", writes=[b_nega])
        P.op("act", lambda e: e.activation(out=nega[:], in_=nega[:], func=AF.Exp), reads=[b_nega], writes=[b_nega])
        P.op("dve", lambda e: e.tensor_scalar(out=nega[:], in0=nega[:], scalar1=-1.0, scalar2=None, op0=ALU.mult),
             reads=[b_nega], writes=[b_nega])
        wb, bwb = load_group(C_G, 16)
        gsb = SA.sb([128, 64], F32, "gsb")
        b_gsb = Buf()
        xg = SA.sb([128, 4, 8], F32, "xg")
        ax = SA.sb([128, 4, 8], F32, "ax")
        b_xg, b_ax = Buf(), Buf()
        for blk in range(NB):
            pzt, bpz, _ = next_pz()
            pv = pzt[:, 0:64].rearrange("p (i c) -> p i c", c=16)
            for i in range(4):
                tt = blk * 4 + i
                for k in range(8):
                    mm(pzt[:, i * 16:(i + 1) * 16], xnT[:, k, tt * 128:(tt + 1) * 128], wb[:, k, 0:16], k == 0, k == 7,
                       [bwb, b_xnT[blk]], [bpz])
            gsl = gall[:, blk * 4:(blk + 1) * 4, :]
            P.op("dve", lambda e, pzt=pzt: e.tensor_copy(out=gsb[:], in_=pzt[:, 0:64]), reads=[bpz], writes=[b_gsb])
            pv = gsb[:, :].rearrange("p (i c) -> p i c", c=16)
            bpz = b_gsb
            P.op("dve", lambda e, pv=pv: e.tensor_tensor(out=xg[:], in0=pv[:, :, 0:8], in1=dtb[:, :].unsqueeze(1).to_broadcast([128, 4, 8]), op=ALU.add),
                 reads=[bpz, b_dtb], writes=[b_xg])
            P.op("act", lambda e: e.activation(out=ax[:], in_=xg[:], func=AF.Abs), reads=[b_xg], writes=[b_ax])
            P.op("act", lambda e: e.activation(out=ax[:], in_=ax[:], func=AF.Exp, scale=-1.0), reads=[b_ax], writes=[b_ax])
            P.op("act", lambda e: e.activation(out=ax[:], in_=ax[:], func=AF.Ln, bias=1.0), reads=[b_ax], writes=[b_ax])
            P.op("dve", lambda e: e.tensor_single_scalar(out=xg[:], in_=xg[:], scalar=0.0, op=ALU.max), reads=[b_xg], writes=[b_xg])
            P.op("dve", lambda e: e.tensor_tensor(out=xg[:], in0=xg[:], in1=ax[:], op=ALU.add), reads=[b_xg, b_ax], writes=[b_xg])
            P.op("dve", lambda e, gsl=gsl: e.tensor_tensor(out=gsl[:, :, 0:8], in0=xg[:], in1=nega[:, :].unsqueeze(1).to_broadcast([128, 4, 8]), op=ALU.mult),
                 reads=[b_xg, b_nega], writes=[b_gall[blk]])
            P.op("act", lambda e, gsl=gsl, pv=pv: e.activation(out=gsl[:, :, 8:16], in_=pv[:, :, 8:16], func=AF.Sigmoid),
                 reads=[bpz], writes=[b_gall[blk]])
        dbg_dump("gall", gall[:], b_gall, [128, NT, 16], F32)

        chk("stopA3")
        og = SA.ring(2, [128, 4, 256], BF16, "og")
        ogc = [0]
        for half in range(2):
            wb, bwb = load_group(C_GG + half * 256, 256)
            for blk in range(NB):
                ot, bo, tago = og[ogc[0] % 2]
                ogc[0] += 1
                for i in range(4):
                    tt = blk * 4 + i
                    pzt, bpz, _ = next_pz()
                    for k in range(8):
                        mm(pzt[:, 0:256], xnT[:, k, tt * 128:(tt + 1) * 128], wb[:, k, 0:256], k == 0, k == 7, [bwb, b_xnT[blk]], [bpz])
                    P.op("act", lambda e, ot=ot, pzt=pzt, i=i: e.activation(out=ot[:, i, :], in_=pzt[:, 0:256], func=AF.Silu),
                         reads=[bpz], writes=[bo])
                P.dma("sp", lambda e, ot=ot, blk=blk, half=half: e.dma_start(
                    out=sgdn_s[blk * 512:(blk + 1) * 512, half * 256:(half + 1) * 256].rearrange("(i p) c -> p i c", p=128), in_=ot[:]),
                    tago, reads=[bo])
        ofm = SA.ring(2, [128, 512], BF16, "ofm")
        ofc = [0]

        def fm_group_to_dram(c0, func, dst_fn):
            wb, bwb = load_group(c0, 256)
            for m in range(2):
                for blk in range(NB):
                    pzt, bpz, _ = next_pz()
                    proj_fm(wb, bwb, m * 128, 128, blk, pzt[:], bpz)
                    ot, bo, tago = ofm[ofc[0] % 2]
                    ofc[0] += 1
                    if func is None:
                        evac(ot[:], pzt[:], [bpz], [bo])
                    else:
                        P.op("act", lambda e, ot=ot, pzt=pzt: e.activation(out=ot[:], in_=pzt[:], func=func), reads=[bpz], writes=[bo])
                    P.dma("sp", lambda e, ot=ot, m=m, blk=blk: e.dma_start(out=dst_fn(m, blk), in_=ot[:]), tago, reads=[bo])

        for g in range(8):
            fm_group_to_dram(C_GA + g * 256, AF.Sigmoid, lambda m, blk, g=g: sg_s[2 * g + m, :, blk * 512:(blk + 1) * 512])
        chk("stopA4")
        zpad = SA.sb([128, 12, 2], BF16, "zpad")
        b_zpad = Buf()
        P.op("pool", lambda e: e.memset(zpad[:], 0.0), writes=[b_zpad])
        P.dma("sp", lambda e: e.dma_start(out=zq_s[:, :, 0:2].rearrange("j p c -> p j c"), in_=zpad[:]), "zp", reads=[b_zpad])
        P.dma("sp", lambda e: e.dma_start(out=zq_s[:, :, T + 2:T + 4].rearrange("j p c -> p j c"), in_=zpad[:]), "zp", reads=[b_zpad])
        for g in range(6):
            fm_group_to_dram(C_QKV + g * 256, None, lambda m, blk, g=g: zq_s[2 * g + m, :, 2 + blk * 512:2 + (blk + 1) * 512])

        P.barrier()
        if "dramA" in dbg:
            for nm, src, shp in (("sgdn_s", sgdn_s, [T, 512]), ("sg_s", sg_s, [16, 128, T]), ("zq_s", zq_s, [12, 128, T + 4])):
                d = dbg_tensor(nm, shp, BF16)
                P.dma("sp", lambda e, d=d, src=src: e.dma_start(out=d, in_=src), uname("dbg"))
            P.barrier()
        P.emit()
        SA.close()
        chk("stopA")

        SC = Scope()
        qnc, b_qnc = load_cols(SC, q_norm_w, 256, "qnc")
        kvnc, b_kvnc = load_cols(SC, kv_norm_w, 128, "kvnc")
        stq = SC.sb([128, 2, 768], F32, "stq")
        stkv = SC.sb([128, 1024], F32, "stkv")
        b_stq, b_stkv = Buf(), Buf()
        P.dma("sp", lambda e: e.dma_start(out=stq[:], in_=w_uq.rearrange("(k p) n -> p k n", p=128)), P.new_tag(), writes=[b_stq])
        P.dma("sp", lambda e: e.dma_start(out=stkv[:], in_=w_ukv[:, :]), P.new_tag(), writes=[b_stkv])
        wqA = SC.sb([128, 2, 8, 96], BF16, "wqA")
        wqB = SC.sb([128, 2, 8, 32], BF16, "wqB")
        wkK = SC.sb([128, 8, 96], BF16, "wkK")
        wkV = SC.sb([128, 8, 64], BF16, "wkV")
        b_wq, b_wk = Buf(), Buf()
        for k in range(2):
            sv = stq[:, k, :].rearrange("p (h c) -> p h c", c=96)
            sc = qnc[:, k:k + 1]
            P.op("dve", lambda e, k=k, sv=sv, sc=sc: e.tensor_scalar(out=wqA[:, k, :, 64:96], in0=sv[:, :, 64:96], scalar1=sc, scalar2=None, op0=ALU.mult),
                 reads=[b_stq, b_qnc], writes=[b_wq])
            P.op("dve", lambda e, k=k, sv=sv, sc=sc: e.tensor_scalar(out=wqA[:, k, :, 0:64], in0=sv[:, :, 0:64], scalar1=sc, scalar2=None, op0=ALU.mult),
                 reads=[b_stq, b_qnc], writes=[b_wq])
            P.op("dve", lambda e, k=k, sv=sv, sc=sc: e.tensor_scalar(out=wqB[:, k, :, 0:16], in0=sv[:, :, 80:96], scalar1=sc, scalar2=-1.0, op0=ALU.mult, op1=ALU.mult),
                 reads=[b_stq, b_qnc], writes=[b_wq])
            P.op("dve", lambda e, k=k, sv=sv, sc=sc: e.tensor_scalar(out=wqB[:, k, :, 16:32], in0=sv[:, :, 64:80], scalar1=sc, scalar2=None, op0=ALU.mult),
                 reads=[b_stq, b_qnc], writes=[b_wq])
        kvv = stkv[:, :].rearrange("p (h c) -> p h c", c=128)
        P.op("pool", lambda e: e.memset(wkK[:], 0.0), writes=[b_wk])
        P.op("dve", lambda e: e.tensor_scalar(out=wkK[:, :, 0:64], in0=kvv[:, :, 0:64], scalar1=kvnc[:, 0:1], scalar2=None, op0=ALU.mult),
             reads=[b_stkv, b_kvnc, b_wk], writes=[b_wk])
        P.op("dve", lambda e: e.tensor_scalar(out=wkV[:], in0=kvv[:, :, 64:128], scalar1=kvnc[:, 0:1], scalar2=None, op0=ALU.mult),
             reads=[b_stkv, b_kvnc], writes=[b_wk])

        V_all = SC.sb([128, NT, 8, 64], BF16, "V_all")
        b_V = Buf()
        VAr = SC.ring(2, [128, NT, 128], BF16, "VA")
        for (va, bva, _) in VAr:
            P.op("pool", lambda e, va=va: e.memset(va[:], 1.0), writes=[bva])
        KTr = SC.ring(2, [96, T], BF16, "KT")
        QTr = SC.ring(2, [96, T], BF16, "QT")
        pTr = SC.ring(3, [128, 2, 512], BF16, "pTs")
        rDr = SC.ring(2, [128, 512], F32, "rD")
        onr = SC.ring(2, [64, 512], BF16, "on")
        tq1 = SC.sb([96, 256], F32, "tq1")[64:96, :]
        tq2 = SC.sb([96, 256], F32, "tq2")[64:96, :]
        b_tq1, b_tq2 = Buf(), Buf()
        ps_s = SC.ring(2, [128, 2, 512], F32, "ps_s", psum=True)
        po_r = SC.ring(2, [128, 512], F32, "po", psum=True)
        pkv = SC.ps([128, 512], F32, "pkv")
        b_pkv = Buf()
        pqa = SC.ps([96, 2, 256], F32, "pqa")
        b_pqa = Buf()
        for t in range(NT):
            mm(pkv[:], ckvnT[:, t * 128:(t + 1) * 128], wkV[:].rearrange("p h c -> p (h c)"), True, True, [b_ckvn[t // 4], b_wk], [b_pkv])
            P.op("act", lambda e, t=t: e.copy(out=V_all[:, t, :, :], in_=pkv[:].rearrange("p (h c) -> p h c", c=64)),
                 reads=[b_pkv], writes=[b_V])
        sm_scale = 1.0 / math.sqrt(96.0)

        def build_head(h):
            KT, bKT, _ = KTr[h % 2]
            QT, bQT, _ = QTr[h % 2]
            va, bva, _ = VAr[h % 2]
            P.op("dve", lambda e: e.tensor_copy(out=KT[64:96, :], in_=kropeT), reads=b_krope, writes=[bKT])
            P.op("dve", lambda e: e.tensor_copy(out=va[:, :, 0:64], in_=V_all[:, :, h, :]), reads=[b_V], writes=[bva])
            yield
            for blk in range(NB):
                bs = slice(blk * 512, (blk + 1) * 512)
                mm(pkv[0:96, :], wkK[:, h, :], ckvnT[:, bs], True, True, [b_wk, b_ckvn[blk]], [b_pkv])
                P.op("act", lambda e, bs=bs: e.copy(out=KT[0:64, bs], in_=pkv[0:64, :]), reads=[b_pkv], writes=[bKT])
                for hf in range(2):
                    hs = slice(blk * 512 + hf * 256, blk * 512 + (hf + 1) * 256)
                    for k in range(2):
                        mm(pqa[:, 0, :], wqA[:, k, h, :], cqnT[:, k, hs], k == 0, k == 1, [b_wq, b_cqn[blk]], [b_pqa])
                    for k in range(2):
                        mm(pqa[64:96, 1, :], wqB[:, k, h, :], cqnT[:, k, hs], k == 0, k == 1, [b_wq, b_cqn[blk]], [b_pqa])
                    P.op("dve", lambda e, hs=hs: e.tensor_copy(out=QT[0:64, hs], in_=pqa[0:64, 0, :]), reads=[b_pqa], writes=[bQT])
                    P.op("dve", lambda e, hs=hs: e.tensor_tensor(out=tq1, in0=pqa[64:96, 0, :], in1=cosT[:, hs], op=ALU.mult),
                         reads=[b_pqa, b_rope], writes=[b_tq1])
                    P.op("dve", lambda e, hs=hs: e.tensor_tensor(out=tq2, in0=pqa[64:96, 1, :], in1=sinT[:, hs], op=ALU.mult),
                         reads=[b_pqa, b_rope], writes=[b_tq2])
                    P.op("dve", lambda e, hs=hs: e.tensor_tensor(out=QT[64:96, hs], in0=tq1, in1=tq2, op=ALU.add),
                         reads=[b_tq1, b_tq2], writes=[bQT])
                    yield

        def run_all(g):
            for _ in g:
                pass

        run_all(build_head(0))
        if "QT0" in dbg:
            dbg_dump("QT0", QTr[0][0][:], [QTr[0][1]], [96, T], BF16)
            dbg_dump("KT0", KTr[0][0][:], [KTr[0][1]], [96, T], BF16)
        NP2 = NT // 2
        units = [(h, qb, kp) for h in range(8) for qb in range(NB) for kp in range(NP2)]
        occ = {}

        def emit_S(u):
            h, qb, kp = units[u]
            KT, bKT, _ = KTr[h % 2]
            QT, bQT, _ = QTr[h % 2]
            pss, bpss, _ = ps_s[u % 2]
            for j in range(2):
                kt = 2 * kp + j
                mm(pss[:, j, :], KT[:, kt * 128:(kt + 1) * 128], QT[:, qb * 512:(qb + 1) * 512], True, True, [bKT, bQT], [bpss])

        def emit_exp(u):
            pss, bpss, _ = ps_s[u % 2]
            pT_, bpT_, _ = pTr[u % 3]
            P.op("act", lambda e: e.activation(out=pT_[:], in_=pss[:], func=AF.Exp, scale=sm_scale), reads=[bpss], writes=[bpT_])

        def emit_PV(u):
            h, qb, kp = units[u]
            va, bva, _ = VAr[h % 2]
            pT_, bpT_, _ = pTr[u % 3]
            oi = (h * NB + qb)
            po, bpo, _ = po_r[oi % 2]
            for j in range(2):
                kt = 2 * kp + j
                mm(po[:], va[:, kt, :], pT_[:, j, :], kt == 0, kt == NT - 1, [bva, bpT_], [bpo])
            if kp == NP2 - 1:
                rD, brD, _ = rDr[oi % 2]
                on, bon, tagon = onr[oi % 2]
                P.op("dve", lambda e: e.reciprocal(out=rD[64:128, :], in_=po[64:128, :]), reads=[bpo], writes=[brD])
                P.op("dve", lambda e: e.tensor_tensor(out=on[:], in0=po[0:64, :], in1=rD[64:128, :], op=ALU.mult), reads=[bpo, brD], writes=[bon])
                P.dma("sp", lambda e: e.dma_start(out=oa_s[h, :, qb * 512:(qb + 1) * 512], in_=on[:]), tagon, reads=[bon])

        bgen = None
        emit_S(0)
        for u in range(len(units)):
            h, qb, kp = units[u]
            if qb == 0 and kp == 0 and h + 1 < 8:
                bgen = build_head(h + 1)
            emit_exp(u)
            if u + 1 < len(units):
                h2_, _, _ = units[u + 1]
                if h2_ != h and bgen is not None:
                    run_all(bgen)
                    bgen = None
                emit_S(u + 1)
            emit_PV(u)
            if bgen is not None and kp % 4 == 3:
                try:
                    next(bgen)
                except StopIteration:
                    bgen = None
        P.barrier()
        if "oa_s" in dbg:
            d = dbg_tensor("oa_s", [8, 64, T], BF16)
            P.dma("sp", lambda e: e.dma_start(out=d, in_=oa_s), uname("dbg"))
            P.barrier()
        P.emit()
        SC.close()
        GA.close()
        chk("stopC1")


        SB = Scope()
        obr = SB.ring(2, [128, 128], BF16, "obr")
        obc = [0]
        cwT = SB.sb([128, 12, 5], F32, "cwT")
        b_cwT = Buf()
        tagc = P.new_tag()
        for tap in range(5):
            P.dma("sp", lambda e, tap=tap: e.dma_start(out=cwT[:, :, tap], in_=conv_w[tap].rearrange("(ch c) -> c ch", c=128),
                                                       allow_slow_non_contiguous=True), tagc, writes=[b_cwT])
        gnw = SB.sb([128, 128], F32, "gnw")
        b_gnw = Buf()
        P.dma("sp", lambda e: e.dma_start(out=gnw[:], in_=gdn_norm_w[0:1, :].to_broadcast([128, 128])), P.new_tag(), writes=[b_gnw])
        zcr = SB.ring(2, [128, T + 4], BF16, "zc")
        dgr = SB.ring(2, [128, 5, 128], BF16, "dg")
        qn = SB.sb([128, NT, 128], BF16, "qn")
        kn = SB.sb([128, NT, 128], BF16, "kn")
        vt = SB.sb([128, NT, 128], BF16, "vt")
        b_qkv = [[Buf() for _ in range(NT)] for _ in range(3)]
        o_f = SB.sb([128, NT, 128], BF16, "o_f")
        b_of = [Buf() for _ in range(NT)]
        o_w = SB.sb([128, NT, 128], BF16, "o_w")
        b_ow = [Buf() for _ in range(NT)]
        sgh = SB.sb([128, NT, 128], BF16, "sgh")
        b_sgh = Buf()
        s4r = SB.ring(2, [128, 4, 128], F32, "s4")
        sq4 = SB.sb([128, 4, 128], F32, "sq4")
        b_sq4 = Buf()
        ss4 = SB.sb([128, 4], F32, "ss4")
        b_ss4 = Buf()
        pbank = {d_: [SB.ps([128, 512], F32, "pbk") for _ in range(4)] for d_ in range(2)}
        bbank = {d_: [Buf(excl=True) for _ in range(4)] for d_ in range(2)}
        pcr = [(pbank[d_][0][:, :].rearrange("p (a c) -> p a c", c=128), bbank[d_][0], None) for d_ in range(2)]
        identf = CF("ident")
        onesf = CF("ones")

        def conv_chunk(j, kind, dst, dst_bufs, ci):
            zc, bzc, tagz = zcr[ci % 2]
            dg, bdg, _ = dgr[ci % 2]
            P.dma("sp", lambda e: e.dma_start(out=zc[:], in_=zq_s[j]), tagz, writes=[bzc])
            for tap in range(5):
                P.op("pool", lambda e, tap=tap: e.tensor_scalar(out=dg[:, tap, :], in0=identf, scalar1=cwT[:, j, tap:tap + 1], scalar2=None, op0=ALU.mult),
                     reads=[b_cwT, b_cst], writes=[bdg])
            for blk in range(NB):
                pc, bpc, _ = pcr[blk % 2]
                s4, bs4, _ = s4r[blk % 2]
                for i in range(4):
                    tt = blk * 4 + i
                    for tap in range(5):
                        mm(pc[:, i, :], zc[:, tt * 128 + tap: tt * 128 + tap + 128], dg[:, tap, :], tap == 0, tap == 4, [bzc, bdg], [bpc])
                P.op("act", lambda e, pc=pc, s4=s4: e.activation(out=s4[:], in_=pc[:], func=AF.Silu), reads=[bpc], writes=[bs4])
                if kind == "v":
                    P.op("pool", lambda e, s4=s4, blk=blk: e.tensor_copy(out=dst[:, blk * 4:(blk + 1) * 4, :], in_=s4[:]),
                         reads=[bs4], writes=[dst_bufs[blk * 4 + i] for i in range(4)])
                else:
                    for i in range(4):
                        P.op("act", lambda e, s4=s4, i=i: e.activation(out=sq4[:, i, :], in_=s4[:, i, :], func=AF.Square, accum_out=ss4[:, i:i + 1]),
                             reads=[bs4], writes=[b_sq4, b_ss4])
                    P.op("act", lambda e: e.activation(out=ss4[:], in_=ss4[:], func=AF.Sqrt, bias=EPS), reads=[b_ss4], writes=[b_ss4])
                    P.op("dve", lambda e: e.reciprocal(out=ss4[:], in_=ss4[:]), reads=[b_ss4], writes=[b_ss4])
                    for i in range(4):
                        tt = blk * 4 + i
                        sc2 = (128.0 ** -0.5) if kind == "q" else 1.0
                        P.op("dve", lambda e, s4=s4, i=i, tt=tt, sc2=sc2: e.tensor_scalar(out=dst[:, tt, :], in0=s4[:, i, :], scalar1=ss4[:, i:i + 1],
                                                                                         scalar2=sc2, op0=ALU.mult, op1=ALU.mult),
                             reads=[bs4, b_ss4], writes=[dst_bufs[tt]])

        RD = 4

        def mk(shape, dt, name):
            return [[SB.sb(shape, dt, name), Buf()] for _ in range(RD)]

        IB = {}
        for d_ in range(2):
            IB[d_] = dict(
                kqT=mk([128, 2, 128], BF16, "kqT"), gv=mk([128, 4], F32, "gv"), gcs=mk([128, 6], F32, "gcs"),
                exi=mk([128, 4], F32, "exi"), ex=mk([128, 4], F32, "ex"), dgc=mk([128, 128], F32, "dgc"),
                dec=mk([128, 128], F32, "dec"), Lq=mk([128, 2, 128], BF16, "Lq"),
                LqT=mk([128, 2, 128], BF16, "LqT"), yb=mk([128, 256], BF16, "yb"), yb2=mk([128, 256], BF16, "yb2"),
                W=[mk([128, 2, 128], BF16, "W") for _ in range(2)], WpI=[mk([128, 128], BF16, "WpI") for _ in range(2)], bk=mk([128, 1], F32, "bk"),
                qg=mk([128, 128], BF16, "qg"), kdec=mk([128, 2, 128], BF16, "kdec"),
                QnT=mk([128, 128], BF16, "QnT"), O0=mk([128, 128], F32, "O0"), MT=mk([128, 2, 128], BF16, "MT"), Cc=mk([128, 2, 128], F32, "Cc"),
                Sbs=[[SB.sb([128, 128], BF16, "Sb"), Buf()] for _ in range(2)],
            )
            pb, bb = pbank[d_], bbank[d_]
            IB[d_].update(
                p_B1=[pb[0][:, 0:128], bb[0]], p_gc=[pb[0][:, 128:136], bb[0]],
                p_gq=[pb[1][:, 0:256].rearrange("p (a c) -> p a c", c=128), bb[1]],
                p_kq=[pb[1][:, 256:384].bitcast(BF16).rearrange("p (a c) -> p a c", c=128), bb[1]],
                p_tr=[pb[1][:, 384:512].bitcast(BF16).rearrange("p (a c) -> p a c", c=128), bb[1]],
                p_w=[[pb[2][:, 0:256].rearrange("p (a c) -> p a c", c=128), bb[2]], [pb[0][:, 0:256].rearrange("p (a c) -> p a c", c=128), bb[0]]],
                p_y=[[pb[2][:, 256:512], bb[2]], [pb[0][:, 256:512], bb[0]]],
                p_a=[pb[3][:, 0:128], bb[3]], p_s=[pb[3][:, 128:256], bb[3]],
            )

        kqT_all = SB.sb([128, NT, 2, 128], BF16, "kqT_all")
        b_kqTa = [Buf() for _ in range(NT)]
        PRO = {}
        for d_ in range(2):
            PRO[d_] = (SB.sb([128, 3, NT, 2], F32, "gcs_a"), SB.sb([128, NT, 4], F32, "ex_a"), SB.sb([128, NT], F32, "bk_a"), Buf())
        nb_all = {d_: SB.sb([128, NT], F32, "nb_a") for d_ in range(2)}
        nex_all = {d_: SB.sb([128, NT], F32, "nex_a") for d_ in range(2)}
        exm_all = {d_: SB.sb([128, NT, 2], F32, "exm_a") for d_ in range(2)}
        negidb = SB.sb([128, 128], BF16, "negidb")
        P.op("dve", lambda e: e.tensor_scalar(out=negidb[:], in0=identf, scalar1=-1.0, scalar2=None, op0=ALU.mult), reads=[b_cst], writes=[b_cb])
        gcT_b = {d_: (SB.sb([NT, 128], F32, "gcT"), Buf()) for d_ in range(2)}
        GR_b = {d_: (SB.sb([128, NT, 128], F32, "GR"), Buf()) for d_ in range(2)}
        b_gcd = {d_: Buf() for d_ in range(2)}
        tag_gc = {d_: (P.new_tag(), P.new_tag()) for d_ in range(2)}
        gv_a = SB.sb([128, NT, 4], F32, "gv_a")
        exi_a = SB.sb([128, NT, 4], F32, "exi_a")
        b_gva, b_exia = Buf(), Buf()

        def prologue(h, d_):
            gcs_a, ex_a, bk_a, b_pro = PRO[d_]
            g_all = gall[:, :, d_ * 4 + h]
            beta_all = gall[:, :, 8 + d_ * 4 + h]
            P.op("dve", lambda e: e.tensor_copy(out=gv_a[:, :, 0], in_=g_all), reads=b_gall, writes=[b_gva])
            P.op("dve", lambda e: e.tensor_copy(out=gv_a[:, :, 1], in_=g_all), reads=b_gall, writes=[b_gva])
            P.op("dve", lambda e: e.tensor_scalar(out=gv_a[:, :, 2], in0=g_all, scalar1=CF("misc", cols=slice(0, 1)), scalar2=None, op0=ALU.mult),
                 reads=b_gall + [b_cst], writes=[b_gva])
            P.op("dve", lambda e: e.tensor_scalar(out=gv_a[:, :, 3], in0=g_all, scalar1=CF("misc", cols=slice(1, 2)), scalar2=None, op0=ALU.mult),
                 reads=b_gall + [b_cst], writes=[b_gva])
            pb, bb = pbank[d_][0], bbank[d_][0]
            mcum = CF("MUi") if d_ == 0 else CF("MLi")
            mm(pb[:, 0:2 * NT].rearrange("p (t c) -> p t c", c=2), mcum, gv_a[:, :, 0:2], True, True, [b_gva, b_cst], [bb])
            mm(pb[:, 2 * NT:4 * NT].rearrange("p (t c) -> p t c", c=2), CF("BD"), gv_a[:, :, 0:2], True, True, [b_gva, b_cst], [bb])
            mm(pb[:, 4 * NT:6 * NT].rearrange("p (t c) -> p t c", c=2), onesf, gv_a[:, :, 2:4], True, True, [b_gva, b_cst], [bb])
            P.op("act", lambda e: e.copy(out=gcs_a[:], in_=pb[:, 0:6 * NT].rearrange("p (a t c) -> p a t c", a=3, c=2)), reads=[bb], writes=[b_pro])
            P.op("dve", lambda e: e.tensor_copy(out=exi_a[:, :, 0], in_=gcs_a[:, 0, :, 0]), reads=[b_pro], writes=[b_exia])
            P.op("dve", lambda e: e.tensor_tensor(out=exi_a[:, :, 1], in0=gcs_a[:, 1, :, 0], in1=gcs_a[:, 0, :, 0], op=ALU.subtract), reads=[b_pro], writes=[b_exia])
            P.op("dve", lambda e: e.tensor_copy(out=exi_a[:, :, 2:4], in_=gcs_a[:, 2, :, :]), reads=[b_pro], writes=[b_exia])
            P.op("act", lambda e: e.activation(out=ex_a[:], in_=exi_a[:], func=AF.Exp), reads=[b_exia], writes=[b_pro])
            P.op("dve", lambda e: e.tensor_tensor(out=bk_a[:], in0=beta_all, in1=ex_a[:, :, 0], op=ALU.mult), reads=b_gall + [b_pro], writes=[b_pro])
            P.op("dve", lambda e: e.tensor_scalar(out=nb_all[d_][:], in0=beta_all, scalar1=-1.0, scalar2=None, op0=ALU.mult), reads=b_gall + [b_pro], writes=[b_pro])
            P.op("dve", lambda e: e.tensor_scalar(out=nex_all[d_][:], in0=ex_a[:, :, 0], scalar1=-1.0, scalar2=None, op0=ALU.mult), reads=[b_pro], writes=[b_pro])
            for c in range(2):
                P.op("dve", lambda e, c=c: e.tensor_scalar(out=exm_all[d_][:, :, c], in0=ex_a[:, :, 1], scalar1=CF("misc", cols=slice(c, c + 1)), scalar2=None,
                                                           op0=ALU.mult), reads=[b_pro, b_cst], writes=[b_pro])
            gcT, b_gcT = gcT_b[d_]
            GR, b_GR = GR_b[d_]
            P.op("pe", lambda e: e.transpose(out=pb[0:NT, 0:128], in_=gcs_a[:, 0, :, 0], identity=identf), reads=[b_pro, b_cst], writes=[bb])
            P.op("act", lambda e: e.copy(out=gcT[:], in_=pb[0:NT, 0:128]), reads=[bb], writes=[b_gcT])
            P.dma("sp", lambda e: e.dma_start(out=gcd_s[d_], in_=gcT[:]), tag_gc[d_][0], reads=[b_gcT], writes=[b_gcd[d_]])
            P.dma("sp", lambda e: e.dma_start(out=GR[:].rearrange("p t c -> p (t c)"),
                                              in_=gcd_s[d_:d_ + 1].rearrange("o t c -> o (t c)").to_broadcast([128, NT * 128])),
                  tag_gc[d_][1], reads=[b_gcd[d_]], writes=[b_GR])
            P.op("dve", lambda e: e.tensor_tensor(out=GR[:], in0=GR[:], in1=(CF("BIGf") if d_ == 0 else CF("BIGb")).unsqueeze(1).to_broadcast([128, NT, 128]),
                                                  op=ALU.add), reads=[b_GR, b_cst], writes=[b_GR])

        def solve(h, d_, t, slot):
            B_ = IB[d_]
            g = gall[:, t, d_ * 4 + h: d_ * 4 + h + 1]
            beta = gall[:, t, 8 + d_ * 4 + h: 8 + d_ * 4 + h + 1]
            bg = b_gall[t // 4]
            kqT = kqT_all[:, t, :, :]
            b_kqT = b_kqTa[t]
            gcs_a, ex_a, bk_a, b_pro = PRO[d_]
            gc_ap = gcs_a[:, 0, t, 0:1]
            ex = ex_a[:, t, :]
            b_ex = b_pro
            b_gcs = b_pro
            GR, b_GR = GR_b[d_]
            dec, b_dec = B_["dec"][slot]
            P.op("act", lambda e: e.activation(out=dec[:], in_=GR[:, t, :], func=AF.Exp, scale=-1.0, bias=gc_ap), reads=[b_GR, b_gcs], writes=[b_dec])
            yield
            p_gq, b_pgq = B_["p_gq"]
            mm(p_gq[:, 0, :], kqT[:, 0, :], kqT[:, 0, :], True, False, [b_kqT], [b_pgq])
            mm(p_gq[:, 0, :], negidb[:], identb, False, True, [b_cb], [b_pgq])
            mm(p_gq[:, 1, :], kqT[:, 1, :], kqT[:, 0, :], True, True, [b_kqT], [b_pgq])
            Lq, b_Lq = B_["Lq"][slot]
            nbeta = nb_all[d_][:, t:t + 1]
            P.op("dve", lambda e: e.scalar_tensor_tensor(out=Lq[:, 0, :], in0=p_gq[:, 0, :], scalar=nbeta, in1=dec[:], op0=ALU.mult, op1=ALU.mult),
                 reads=[b_pgq, b_pro, b_dec], writes=[b_Lq])
            P.op("dve", lambda e: e.tensor_tensor(out=Lq[:, 1, :], in0=p_gq[:, 1, :], in1=dec[:], op=ALU.mult), reads=[b_pgq, b_dec], writes=[b_Lq])
            yield
            p_tr, b_ptr = B_["p_tr"]
            LqT, b_LqT = B_["LqT"][slot]
            P.op("pe", lambda e: e.transpose(out=p_tr[:, 0, :], in_=Lq[:, 0, :], identity=identb), reads=[b_Lq, b_cb], writes=[b_ptr])
            P.op("pe", lambda e: e.transpose(out=p_tr[:, 1, :], in_=Lq[:, 1, :], identity=identb), reads=[b_Lq, b_cb], writes=[b_ptr])
            P.op("dve", lambda e: e.tensor_copy(out=LqT[:], in_=p_tr), reads=[b_ptr], writes=[b_LqT])
            WpI, b_WpI = B_["WpI"][0][slot]
            P.op("dve", lambda e, WpI=WpI: e.tensor_tensor(out=WpI[:], in0=LqT[:, 0, :], in1=identb, op=ALU.add), reads=[b_LqT, b_cb], writes=[b_WpI])
            ybs = [B_["yb"][slot], B_["yb2"][slot]]
            yb, b_yb = ybs[0]
            bk_ap = bk_a[:, t:t + 1]
            P.op("act", lambda e, yb=yb: e.activation(out=yb[:, 0:128], in_=vt[:, t, :], func=AF.Copy, scale=beta), reads=[b_qkv[2][t], bg], writes=[b_yb])
            P.op("act", lambda e, yb=yb: e.activation(out=yb[:, 128:256], in_=kn[:, t, :], func=AF.Copy, scale=bk_ap), reads=[b_qkv[1][t], b_pro], writes=[b_yb])
            yield
            p_y, b_py = B_["p_y"][slot % 2]
            p_w, b_pw = B_["p_w"][slot % 2]
            Wc, WcT, b_Wc = Lq[:, 0, :], LqT[:, 0, :], [b_Lq, b_LqT]
            for lev in range(6):
                yn, b_yn = ybs[(lev + 1) % 2]
                mm(p_y, WpI[:], yb[:], True, True, [b_WpI, b_yb], [b_py])
                if lev < 5:
                    Wn, b_Wn = B_["W"][lev % 2][slot]
                    WpIn, b_WpIn = B_["WpI"][(lev + 1) % 2][slot]
                    if lev < 4:
                        mm(p_w[:, 0, :], WcT, Wc, True, True, b_Wc, [b_pw])
                    mm(p_w[:, 1, :], Wc, WcT, True, True, b_Wc, [b_pw])
                P.op("act", lambda e, yn=yn: e.copy(out=yn[:], in_=p_y), reads=[b_py], writes=[b_yn])
                if lev < 5:
                    P.op("dve", lambda e, WpIn=WpIn: e.tensor_tensor(out=WpIn[:], in0=p_w[:, 1, :], in1=identf, op=ALU.add),
                         reads=[b_pw, b_cst], writes=[b_WpIn])
                    if lev < 4:
                        P.op("dve", lambda e, Wn=Wn: e.tensor_copy(out=Wn[:], in_=p_w), reads=[b_pw], writes=[b_Wn])
                    Wc, WcT, b_Wc = Wn[:, 0, :], Wn[:, 1, :], [b_Wn]
                    WpI, b_WpI = WpIn, b_WpIn
                yb, b_yb = yn, b_yn
                yield
            sol, b_sol = B_["yb"][slot]
            nqg, b_nqg = B_["qg"][slot]
            kdec, b_kdec = B_["kdec"][slot]
            QnT, b_QnT = B_["QnT"][slot]
            O0, b_O0 = B_["O0"][slot]
            MT, b_MT = B_["MT"][slot]
            Cc, b_Cc = B_["Cc"][slot]
            nex0 = nex_all[d_][:, t:t + 1]
            P.op("act", lambda e: e.activation(out=nqg[:], in_=qn[:, t, :], func=AF.Copy, scale=nex0), reads=[b_qkv[0][t], b_pro], writes=[b_nqg])
            for c in range(2):
                P.op("act", lambda e, c=c: e.activation(out=kdec[:, c, :], in_=kn[:, t, :], func=AF.Copy, scale=exm_all[d_][:, t, c:c + 1]),
                     reads=[b_qkv[1][t], b_pro], writes=[b_kdec])
            if "noA" in dbg:
                return
            mm(p_w[:, 0, :], sol[:, 128:256], LqT[:, 1, :], True, False, [b_sol, b_LqT], [b_pw])
            mm(p_w[:, 0, :], nqg[:], identb, False, True, [b_nqg, b_cb], [b_pw])
            mm(p_w[:, 1, :], LqT[:, 1, :], sol[:, 0:128], True, True, [b_sol, b_LqT], [b_pw])
            P.op("dve", lambda e: e.tensor_copy(out=QnT[:], in_=p_w[:, 0, :]), reads=[b_pw], writes=[b_QnT])
            P.op("act", lambda e: e.copy(out=O0[:], in_=p_w[:, 1, :]), reads=[b_pw], writes=[b_O0])
            yield
            if "noB" in dbg:
                return
            for c in range(2):
                cs = slice(c * 64, (c + 1) * 64)
                mm(p_w[:, c, :], sol[:, 128:256], kdec[:, c, :], True, True, [b_sol, b_kdec], [b_pw])
            for c in range(2):
                P.op("dve", lambda e, c=c: e.scalar_tensor_tensor(out=MT[:, c, :], in0=identf, scalar=ex[:, 2 + c:3 + c], in1=p_w[:, c, :],
                                                                  op0=ALU.mult, op1=ALU.subtract), reads=[b_pw, b_ex, b_cst], writes=[b_MT])
            yield
            if "noC" in dbg:
                return
            for c in range(2):
                cs = slice(c * 64, (c + 1) * 64)
                mm(p_w[:, c, :], kdec[:, c, :], sol[:, 0:128], True, True, [b_sol, b_kdec], [b_pw])
            P.op("act", lambda e: e.copy(out=Cc[:], in_=p_w), reads=[b_pw], writes=[b_Cc])
            yield

        def scan(h, d_, t, slot):
            B_ = IB[d_]
            QnT, b_QnT = B_["QnT"][slot]
            O0, b_O0 = B_["O0"][slot]
            MT, b_MT = B_["MT"][slot]
            Cc, b_Cc = B_["Cc"][slot]
            p_s, b_ps = B_["p_s"]
            p_o, b_po = B_["p_a"]
            if "noScan" in dbg:
                return
                yield
            for c in ((0, 1) if d_ == 0 else (1, 0)):
                cs = slice(c * 64, (c + 1) * 64)
                Sb, b_Sb = B_["Sbs"][Sidx[d_] % 2]
                Sn, b_Sn = B_["Sbs"][(Sidx[d_] + 1) % 2]
                Sidx[d_] += 1
                mm(p_s, MT[:, c, :], Sb[:], True, True, [b_MT, b_Sb], [b_ps])
                P.op("dve", lambda e, c=c, Sn=Sn: e.tensor_tensor(out=Sn[:], in0=p_s, in1=Cc[:, c, :], op=ALU.add), reads=[b_ps, b_Cc], writes=[b_Sn])
                yield
                dst, bdst = (o_f, b_of) if d_ == 0 else (o_w, b_ow)
                if "noPo" not in dbg:
                    mm(p_o[cs, :], QnT[:, cs], Sb[:], True, True, [b_QnT, b_Sb], [b_po])
                    P.op("dve", lambda e, cs=cs, dst=dst: e.tensor_tensor(out=dst[cs, t, :], in0=O0[cs, :], in1=p_o[cs, :], op=ALU.subtract),
                         reads=[b_po, b_O0], writes=[bdst[t]])
                else:
                    P.op("dve", lambda e, cs=cs, dst=dst: e.tensor_copy(out=dst[cs, t, :], in_=O0[cs, :]), reads=[b_O0], writes=[bdst[t]])
                yield

        Sidx = {0: 0, 1: 0}

        fin_bufs = [dict(osum=[SB.sb([128, 128], F32, "osum"), Buf()], ob=[SB.sb([128, 128], BF16, "ob"), Buf()],
                         ssn=[SB.sb([128, 1], F32, "ssn"), Buf()], junk=[SB.sb([128, 128], BF16, "junkb"), Buf()]) for _ in range(4)]

        def final(h, t, fb):
            osum, b_osum = fb["osum"]
            ob, b_ob = fb["ob"]
            ssn, b_ssn = fb["ssn"]
            junkb, b_junkb = fb["junk"]
            p_tr, b_ptr = IB[t % 2]["p_tr"]
            P.op("pool", lambda e: e.tensor_tensor(out=osum[:], in0=o_f[:, t, :], in1=o_w[:, t, :], op=ALU.add), reads=[b_of[t], b_ow[t]], writes=[b_osum])
            yield
            P.op("act", lambda e: e.activation(out=junkb[:], in_=osum[:], func=AF.Square, accum_out=ssn[:]), reads=[b_osum], writes=[b_junkb, b_ssn])
            P.op("act", lambda e: e.activation(out=ssn[:], in_=ssn[:], func=AF.Ln, scale=1.0 / 128, bias=EPS), reads=[b_ssn], writes=[b_ssn])
            P.op("act", lambda e: e.activation(out=ssn[:], in_=ssn[:], func=AF.Exp, scale=-0.5), reads=[b_ssn], writes=[b_ssn])
            yield
            P.op("dve", lambda e: e.scalar_tensor_tensor(out=osum[:], in0=osum[:], scalar=ssn[:, 0:1], in1=gnw[:], op0=ALU.mult, op1=ALU.mult),
                 reads=[b_osum, b_ssn, b_gnw], writes=[b_osum])
            yield
            P.op("pool", lambda e: e.tensor_tensor(out=ob[:], in0=osum[:], in1=sgh[:, t, :], op=ALU.mult), reads=[b_osum, b_sgh], writes=[b_ob])
            yield
            P.op("pe", lambda e: e.transpose(out=p_tr[:, 0, :], in_=ob[:], identity=identb), reads=[b_ob, b_cb], writes=[b_ptr])
            obt, b_obt, tago = obr[obc[0] % 2]
            obc[0] += 1
            P.op("dve", lambda e: e.tensor_copy(out=obt[:], in_=p_tr[:, 0, :]), reads=[b_ptr], writes=[b_obt])
            P.dma("sp", lambda e: e.dma_start(out=ob_s[h, :, t * 128:(t + 1) * 128], in_=obt[:]), tago, reads=[b_obt])
            yield

        IBo = {0: IB[0]["p_a"], 1: IB[1]["p_a"]}

        def run_rr(gens):
            gens = list(gens)
            while gens:
                for g_ in list(gens):
                    try:
                        next(g_)
                    except StopIteration:
                        gens.remove(g_)

        ci = 0
        for h in range(4):
            for d_ in range(2):
                prologue(h, d_)
            conv_chunk(h, "q", qn, b_qkv[0], ci); ci += 1
            conv_chunk(4 + h, "k", kn, b_qkv[1], ci); ci += 1
            conv_chunk(8 + h, "v", vt, b_qkv[2], ci); ci += 1
            P.dma("sp", lambda e, h=h: e.dma_start(out=sgh[:], in_=sgdn_s[:, h * 128:(h + 1) * 128].rearrange("(t p) c -> p t c", p=128)),
                  "sgh", writes=[b_sgh])
            if h == 0:
                dbg_dump("qn", qn[:], b_qkv[0], [128, NT, 128], BF16)
                dbg_dump("kn", kn[:], b_qkv[1], [128, NT, 128], BF16)
                dbg_dump("vt", vt[:], b_qkv[2], [128, NT, 128], BF16)
                chk("stopB1")
            for t in range(NT):
                p_kq, b_pkq = IB[t % 2]["p_kq"]
                P.op("pe", lambda e, t=t, p_kq=p_kq: e.transpose(out=p_kq[:, 0, :], in_=kn[:, t, :], identity=identb), reads=[b_qkv[1][t], b_cb], writes=[b_pkq])
                P.op("pe", lambda e, t=t, p_kq=p_kq: e.transpose(out=p_kq[:, 1, :], in_=qn[:, t, :], identity=identb), reads=[b_qkv[0][t], b_cb], writes=[b_pkq])
                P.op("dve", lambda e, t=t, p_kq=p_kq: e.tensor_copy(out=kqT_all[:, t, :, :], in_=p_kq), reads=[b_pkq], writes=[b_kqTa[t]])
            for d_ in range(2):
                for (Sb_, b_Sb_) in IB[d_]["Sbs"]:
                    P.op("pool", lambda e, Sb_=Sb_: e.memset(Sb_[:], 0.0), writes=[b_Sb_])
            NFL = ([int(x[3:]) for x in dbg if x.startswith("nfl")] or [3])[0]
            tasks = []
            orders = {0: list(range(NT)), 1: list(range(NT - 1, -1, -1))}
            for i in range(NT):
                for d_ in range(2):
                    tasks.append((("solve", d_, i), (lambda d_=d_, i=i: solve(h, d_, orders[d_][i], i % RD)),
                                  ([("scan", d_, i - RD)] if i >= RD else []) + ([("solve", d_, i - NFL)] if i >= NFL else [])))
                for d_ in range(2):
                    tasks.append((("scan", d_, i), (lambda d_=d_, i=i: scan(h, d_, orders[d_][i], i % RD)),
                                  [("solve", d_, i)] + ([("scan", d_, i - 1)] if i >= 1 else [])))
            for t in range(NT):
                tasks.append((("final", t), (lambda t=t: final(h, t, fin_bufs[t % 4])),
                              [("scan", 0, t), ("scan", 1, NT - 1 - t)] + ([("final", t - 4)] if t >= 4 else [])))
            if "noGDN" in dbg:
                tasks = []
            SCAN_REPS = ([int(x[2:]) for x in dbg if x.startswith("sr") and x[2:].isdigit()] or [1])[0]
            done, active, pending = set(), [], list(tasks)
            KACT = ([int(x[4:]) for x in dbg if x.startswith("kact")] or [12])[0]
            while pending or active:
                for tsk in list(pending):
                    if len(active) >= KACT:
                        break
                    if all(dd in done for dd in tsk[2]):
                        active.append((tsk[0], tsk[1]()))
                        pending.remove(tsk)
                active.sort(key=lambda it: 0 if it[0][0] == "scan" else 1)
                for item in list(active):
                    reps = SCAN_REPS if item[0][0] == "scan" else 1
                    for _ in range(reps):
                        try:
                            next(item[1])
                        except StopIteration:
                            active.remove(item)
                            done.add(item[0])
                            break
            if h == 0:
                dbg_dump("o_f", o_f[:], b_of, [128, NT, 128], BF16)
        P.barrier()
        if "o_bT" in dbg:
            d = dbg_tensor("o_bT", [4, 128, T], BF16)
            P.dma("sp", lambda e: e.dma_start(out=d, in_=ob_s), uname("dbg"))
            P.barrier()
        P.emit()
        SB.close()
        chk("stopB")

        S2 = Scope()
        wpa = S2.sb([64, 8, D], BF16, "wpa")
        wpb = S2.sb([128, 4, D], BF16, "wpb")
        wo = S2.sb([128, 8, D], BF16, "wo")
        b_wpa, b_wpb, b_wo = Buf(), Buf(), Buf()
        stg2 = S2.ring(2, [128, 2, D], F32, "stg2")
        si = [0]

        def load_plain(dst_fn, src_fn, n, rows, bdst):
            for j in range(n):
                st_, bst, tag = stg2[si[0] % 2]
                si[0] += 1
                P.dma("sp", lambda e, st_=st_, j=j: e.dma_start(out=st_[0:rows, :, :], in_=src_fn(j)), tag, writes=[bst])
                evac(dst_fn(j), st_[0:rows, :, :], [bst], [bdst])

        load_plain(lambda j: wpa[:, 2 * j:2 * j + 2, :], lambda j: w_proj_a[j * 128:(j + 1) * 128, :].rearrange("(h p) n -> p h n", p=64), 4, 64, b_wpa)
        load_plain(lambda j: wpb[:, 2 * j:2 * j + 2, :], lambda j: w_proj_b[j * 256:(j + 1) * 256, :].rearrange("(k p) n -> p k n", p=128), 2, 128, b_wpb)
        load_plain(lambda j: wo[:, 2 * j:2 * j + 2, :], lambda j: w_out[j * 256:(j + 1) * 256, :].rearrange("(k p) n -> p k n", p=128), 4, 128, b_wo)
        oaTr = S2.ring(2, [64, 8, 512], BF16, "oaT")
        obTr = S2.ring(2, [128, 4, 512], BF16, "obT")
        sgr = S2.ring(2, [128, 16, 512], BF16, "sgt")
        mTr = S2.ring(2, [128, 8, 512], BF16, "mT")
        mar = S2.ring(2, [128, 512], F32, "ma")
        mbr = S2.ring(2, [128, 512], F32, "mb")
        xr2 = S2.ring(4, [128, D], F32, "xr2")
        x1r = S2.ring(2, [128, D], F32, "x1r")
        ppar = S2.ring(2, [128, 512], F32, "ppa", psum=True)
        ppbr = S2.ring(2, [128, 512], F32, "ppb", psum=True)
        pxr = S2.ring(2, [128, 512], F32, "px", psum=True)

        def c2_loads(blk):
            bs = slice(blk * 512, (blk + 1) * 512)
            oaT, boaT, tag1 = oaTr[blk % 2]
            obT, bobT, tag2 = obTr[blk % 2]
            sgt, bsgt, tag3 = sgr[blk % 2]
            P.dma("sp", lambda e: e.dma_start(out=oaT[:], in_=oa_s[:, :, bs].rearrange("h p t -> p h t")), tag1, writes=[boaT])
            P.dma("sp", lambda e: e.dma_start(out=obT[:], in_=ob_s[:, :, bs].rearrange("h p t -> p h t")), tag2, writes=[bobT])
            P.dma("sp", lambda e: e.dma_start(out=sgt[:], in_=sg_s[:, :, bs].rearrange("g p t -> p g t")), tag3, writes=[bsgt])

        c2_loads(0)
        mc = [0]
        xc = [0]
        for blk in range(NB):
            oaT, boaT, tag1 = oaTr[blk % 2]
            obT, bobT, tag2 = obTr[blk % 2]
            sgt, bsgt, tag3 = sgr[blk % 2]
            mT, b_mT, _ = mTr[blk % 2]
            xts = []
            for i in range(4):
                tt = blk * 4 + i
                xt, bx, tagx = xr2[i]
                xts.append((xt, bx))
                P.dma("sp", lambda e, xt=xt, tt=tt: e.dma_start(out=xt[:], in_=x[tt * 128:(tt + 1) * 128, :]), tagx, writes=[bx])
            if blk + 1 < NB:
                c2_loads(blk + 1)
            for m in range(8):
                ms = slice(m * 128, (m + 1) * 128)
                ppa, b_ppa, _ = ppar[m % 2]
                ppb, b_ppb, _ = ppbr[m % 2]
                ma, b_ma, _ = mar[m % 2]
                mb, b_mb, _ = mbr[m % 2]
                for h in range(8):
                    mm(ppa[:], wpa[:, h, ms], oaT[:, h, :], h == 0, h == 7, [b_wpa, boaT], [b_ppa])
                for k in range(4):
                    mm(ppb[:], wpb[:, k, ms], obT[:, k, :], k == 0, k == 3, [b_wpb, bobT], [b_ppb])
                P.op("dve", lambda e, sgt=sgt, m=m, ma=ma, ppa=ppa: e.tensor_tensor(out=ma[:], in0=ppa[:], in1=sgt[:, m, :], op=ALU.mult),
                     reads=[b_ppa, bsgt], writes=[b_ma])
                P.op("dve", lambda e, sgt=sgt, m=m, mb=mb, ppb=ppb: e.tensor_tensor(out=mb[:], in0=ppb[:], in1=sgt[:, 8 + m, :], op=ALU.mult),
                     reads=[b_ppb, bsgt], writes=[b_mb])
                P.op("pool", lambda e, m=m, ma=ma, mb=mb, mT=mT: e.tensor_tensor(out=mT[:, m, :], in0=ma[:], in1=mb[:], op=ALU.add),
                     reads=[b_ma, b_mb], writes=[b_mT])
            for i in range(4):
                tt = blk * 4 + i
                xt, bx = xts[i]
                x1t, bx1, tagx1 = x1r[xc[0] % 2]
                xc[0] += 1
                for half in range(2):
                    px, bpx, _ = pxr[half]
                    for k in range(8):
                        mm(px[:], mT[:, k, i * 128:(i + 1) * 128], wo[:, k, half * 512:(half + 1) * 512], k == 0, k == 7, [b_mT, b_wo], [bpx])
                    P.op("dve", lambda e, x1t=x1t, xt=xt, px=px, half=half: e.tensor_tensor(out=x1t[:, half * 512:(half + 1) * 512], in0=px[:],
                                                                                           in1=xt[:, half * 512:(half + 1) * 512], op=ALU.add),
                         reads=[bpx, bx], writes=[bx1])
                P.dma("sp", lambda e, x1t=x1t, tt=tt: e.dma_start(out=x1_s[tt * 128:(tt + 1) * 128, :], in_=x1t[:]), tagx1, reads=[bx1])
        P.barrier()
        if "x1" in dbg:
            d = dbg_tensor("x1", [T, D], F32)
            P.dma("sp", lambda e: e.dma_start(out=d, in_=x1_s), uname("dbg"))
            P.barrier()
        P.emit()
        S2.close()
        chk("stopC2")

        SD = Scope()
        w1 = SD.sb([128, 8, DFF], BF16, "w1")
        w2 = SD.sb([128, 32, D], BF16, "w2")
        b_w1, b_w2 = Buf(), Buf()
        w1v = w_ff1.rearrange("(k p) n -> p k n", p=128)
        w2v = w_ff2.rearrange("(k p) n -> p k n", p=128)
        b_w1g = [Buf() for _ in range(8)]
        b_w2g = [Buf() for _ in range(8)]
        for gq in range(8):
            P.dma("pool", lambda e, gq=gq: e.dma_start(out=w1[:, :, gq * 512:(gq + 1) * 512], in_=w1v[:, :, gq * 512:(gq + 1) * 512]), "w1g%d" % gq, writes=[b_w1g[gq]])
        for gq in range(8):
            P.dma("pool", lambda e, gq=gq: e.dma_start(out=w2[:, 4 * gq:4 * gq + 4, :], in_=w2v[:, 4 * gq:4 * gq + 4, :]), "w2g%d" % gq, writes=[b_w2g[gq]])
        n2w = SD.sb([128, D], F32, "n2w")
        b_n2w = Buf()
        P.dma("sp", lambda e: e.dma_start(out=n2w[:], in_=norm2_w.rearrange("(o n) -> o n", o=1).to_broadcast([128, D])), P.new_tag(), writes=[b_n2w])
        fnw = SD.sb([128, D], F32, "fnw")
        b_fnw = Buf()
        P.dma("sp", lambda e: e.dma_start(out=fnw[:], in_=final_norm_w[0:1, :].to_broadcast([128, D])), P.new_tag(), writes=[b_fnw])
        TB = 256
        NBD = T // TB
        x1d = SD.ring(4, [128, D], F32, "x1d")
        h2r = SD.ring(2, [128, D], BF16, "h2")
        junkd = SD.sb([128, D], BF16, "junkd")
        b_junkd = Buf()
        sd = SD.sb([128, NT], F32, "sd")
        b_sd = [Buf() for _ in range(NT)]
        sd2 = SD.sb([128, NT], F32, "sd2")
        b_sd2 = [Buf() for _ in range(NT)]
        h2Tr = SD.ring(2, [128, 8, TB], BF16, "h2T")
        aT = SD.sb([128, 32, TB], BF16, "aT")
        b_aT = Buf()
        rr = SD.ring(2, [128, TB], F32, "rr")
        yr = SD.ring(2, [128, D], F32, "yr")
        pTd = SD.ring(2, [128, 8, 128], BF16, "pTd", psum=True)
        pfr = SD.ring(2, [128, TB], F32, "pf", psum=True)
        pyr = SD.ring(2, [128, 512], F32, "py", psum=True)
        xdc = [0]
        NTB = TB // 128
        tiles_of = {}

        def d_front(bd):
            h2T, b_h2T, _ = h2Tr[bd % 2]
            tiles = []
            for i in range(NTB):
                tt = bd * NTB + i
                x1t, bx1, tagx1 = x1d[xdc[0] % 4]
                h2, bh2, _ = h2r[xdc[0] % 2]
                pTt, bpT, _ = pTd[xdc[0] % 2]
                xdc[0] += 1
                tiles.append((tt, x1t, bx1))
                P.dma("sp", lambda e, x1t=x1t, tt=tt: e.dma_start(out=x1t[:], in_=x1_s[tt * 128:(tt + 1) * 128, :]), tagx1, writes=[bx1])
                P.op("act", lambda e, x1t=x1t, tt=tt: e.activation(out=junkd[:], in_=x1t[:], func=AF.Square, accum_out=sd[:, tt:tt + 1]),
                     reads=[bx1], writes=[b_junkd, b_sd[tt]])
                P.op("act", lambda e, tt=tt: e.activation(out=sd[:, tt:tt + 1], in_=sd[:, tt:tt + 1], func=AF.Sqrt, scale=1.0 / D, bias=EPS),
                     reads=[b_sd[tt]], writes=[b_sd[tt]])
                P.op("dve", lambda e, tt=tt: e.reciprocal(out=sd[:, tt:tt + 1], in_=sd[:, tt:tt + 1]), reads=[b_sd[tt]], writes=[b_sd[tt]])
                P.op("dve", lambda e, x1t=x1t, h2=h2, tt=tt: e.scalar_tensor_tensor(out=h2[:], in0=x1t[:], scalar=sd[:, tt:tt + 1], in1=n2w[:], op0=ALU.mult, op1=ALU.mult),
                     reads=[bx1, b_sd[tt], b_n2w], writes=[bh2])
                for k in range(8):
                    P.op("pe", lambda e, k=k, h2=h2, pTt=pTt: e.transpose(out=pTt[:, k, :], in_=h2[:, k * 128:(k + 1) * 128], identity=identb),
                         reads=[bh2, b_cb], writes=[bpT])
                P.op("dve", lambda e, pTt=pTt, i=i, h2T=h2T: e.tensor_copy(out=h2T[:, :, i * 128:(i + 1) * 128], in_=pTt[:]), reads=[bpT], writes=[b_h2T])
            tiles_of[bd] = tiles

        d_front(0)
        for bd in range(NBD):
            h2T, b_h2T, _ = h2Tr[bd % 2]
            for f in range(32):
                pf, bpf, _ = pfr[f % 2]
                r_, br_, _ = rr[f % 2]
                for k in range(8):
                    mm(pf[:], w1[:, k, f * 128:(f + 1) * 128], h2T[:, k, :], k == 0, k == 7, [b_w1g[f // 4], b_h2T], [bpf])
                P.op("act", lambda e, pf=pf, r_=r_: e.activation(out=r_[:], in_=pf[:], func=AF.Relu), reads=[bpf], writes=[br_])
                P.op("pool", lambda e, r_=r_, f=f: e.tensor_tensor(out=aT[:, f, :], in0=r_[:], in1=r_[:], op=ALU.mult), reads=[br_], writes=[b_aT])
            if bd + 1 < NBD:
                d_front(bd + 1)
            for i, (tt, x1t, bx1) in enumerate(tiles_of[bd]):
                yt, byt, tagy = yr[tt % 2]
                for half in range(2):
                    py, bpy, _ = pyr[half]
                    for f in range(32):
                        mm(py[:], aT[:, f, i * 128:(i + 1) * 128], w2[:, f, half * 512:(half + 1) * 512], f == 0, f == 31, [b_aT, b_w2g[f // 4]], [bpy])
                    P.op("dve", lambda e, yt=yt, x1t=x1t, py=py, half=half: e.tensor_tensor(out=yt[:, half * 512:(half + 1) * 512], in0=py[:],
                                                                                           in1=x1t[:, half * 512:(half + 1) * 512], op=ALU.add),
                         reads=[bpy, bx1], writes=[byt])
                P.op("act", lambda e, yt=yt, tt=tt: e.activation(out=junkd[:], in_=yt[:], func=AF.Square, accum_out=sd2[:, tt:tt + 1]),
                     reads=[byt], writes=[b_junkd, b_sd2[tt]])
                P.op("act", lambda e, tt=tt: e.activation(out=sd2[:, tt:tt + 1], in_=sd2[:, tt:tt + 1], func=AF.Sqrt, scale=1.0 / D, bias=EPS),
                     reads=[b_sd2[tt]], writes=[b_sd2[tt]])
                P.op("dve", lambda e, tt=tt: e.reciprocal(out=sd2[:, tt:tt + 1], in_=sd2[:, tt:tt + 1]), reads=[b_sd2[tt]], writes=[b_sd2[tt]])
                P.op("dve", lambda e, yt=yt, tt=tt: e.scalar_tensor_tensor(out=yt[:], in0=yt[:], scalar=sd2[:, tt:tt + 1], in1=fnw[:], op0=ALU.mult, op1=ALU.mult),
                     reads=[byt, b_sd2[tt], b_fnw], writes=[byt])
                P.dma("sp", lambda e, yt=yt, tt=tt: e.dma_start(out=y[tt * 128:(tt + 1) * 128, :], in_=yt[:]), tagy, reads=[byt])
        P.barrier()
        P.emit(final=True)
        SD.close()
        return nc, dbg_out
    except Stop:
        P.barrier()
        P.emit(final=True)
        return nc, dbg_out


_CST = None


def core_inputs(inp, b, T):
    global _CST
    if _CST is None:
        _CST = make_consts()
    f = lambda a: np.ascontiguousarray(a, dtype=np.float32)
    return {
        "x": f(inp["x"][b, :T]),
        "positions": np.ascontiguousarray(inp["positions"][b, :T], dtype=np.int32).reshape(1, T),
        "norm1_w": f(inp["norm1_w"][0]), "w_in": f(inp["w_in"][0]), "q_norm_w": f(inp["q_norm_w"][0]),
        "w_uq": f(inp["w_uq"][0]), "kv_norm_w": f(inp["kv_norm_w"][0]), "w_ukv": f(inp["w_ukv"][0]),
        "conv_w": f(inp["conv_w"][0]),
        "a_log_f": f(inp["a_log_f"]), "dt_bias_f": f(inp["dt_bias_f"]),
        "a_log_b": f(inp["a_log_b"]), "dt_bias_b": f(inp["dt_bias_b"]),
        "gdn_norm_w": f(inp["gdn_norm_w"]),
        "w_proj_a": f(inp["w_proj_a"][0]), "w_proj_b": f(inp["w_proj_b"][0]), "w_out": f(inp["w_out"][0]),
        "norm2_w": f(inp["norm2_w"][0]), "w_ff1": f(inp["w_ff1"][0]), "w_ff2": f(inp["w_ff2"][0]),
        "final_norm_w": f(inp["final_norm_w"]).reshape(1, D),
        "cst": _CST,
    }


def kernel(**inputs):
    T = 4096
    nc, _ = build(T=T)
    maps = [core_inputs(inputs, b, T) for b in range(8)]
    res = run_bass_kernel_spmd(nc, maps, core_ids=list(range(8)))
    return np.stack([np.asarray(r["y"], dtype=np.float32) for r in res.results], axis=0)
```
